# Optimizing a Trainium2 kernel written in Bass

```python
import math
import jax
import jax.numpy as jnp
from jax import lax
import numpy as np

D_MODEL = 2048
BATCH = 4
SEQ = 2048
DEPTH = 4

N_MIXERS = 2
N_A_LAYERS = (DEPTH + 1) // 2
N_B_LAYERS = DEPTH // 2
EPS = 1e-6
NEG = -1e30

A_HEAD_DIM = 128
A_HEADS_PER_GROUP = 4
A_PATTERNS = ((128, 1), (512, 4), (2048, 16))
A_N_GROUPS = len(A_PATTERNS)
A_HEADS = A_HEADS_PER_GROUP * A_N_GROUPS
A_QKV_W = A_HEADS * A_HEAD_DIM
A_OUT_W = A_HEADS_PER_GROUP * A_HEAD_DIM
QBLK = 128

CHUNK = 128
B_GROUPS = 12
B_GROUP_W = 128
B_W = B_GROUPS * B_GROUP_W

MEM_LEN = 256
MEM_HEADS = 4
MEM_HEAD_DIM = 128
MEM_W = MEM_HEADS * MEM_HEAD_DIM

A_IN = 3 * A_QKV_W + MEM_W
A_OUT_IN = A_OUT_W + MEM_W
B_IN = 2 * B_W + MEM_W
B_OUT_IN = B_W + MEM_W

FF = 5632
CONV_W = 3

kernel_name = "hybrid_dilated_sgu_memory_encoder"


def rmsnorm(x, g):
    xf = x.astype(jnp.float32)
    y = xf * lax.rsqrt(jnp.mean(xf * xf, axis=-1, keepdims=True) + EPS)
    return (y * g.astype(jnp.float32)).astype(x.dtype)


def alibi_slopes():
    return (2.0 ** (-8.0 * (np.arange(A_HEADS) + 1) / A_HEADS)).astype(np.float32)


def dilated_window_attention(q, k, v, dilation, n_side, slopes):
    B, S, H, E = q.shape
    L = S // dilation
    nblk = -(-L // QBLK)
    Lp = nblk * QBLK
    W = QBLK + 2 * n_side
    qs = q.reshape(B, L, dilation, H, E)
    ks = k.reshape(B, L, dilation, H, E)
    vs = v.reshape(B, L, dilation, H, E)
    qb = jnp.pad(qs, ((0, 0), (0, Lp - L), (0, 0), (0, 0), (0, 0)))
    qb = qb.reshape(B, nblk, QBLK, dilation, H, E)
    pad_k = ((0, 0), (n_side, n_side + Lp - L), (0, 0), (0, 0), (0, 0))
    kp = jnp.pad(ks, pad_k)
    vp = jnp.pad(vs, pad_k)
    idx = np.arange(nblk)[:, None] * QBLK + np.arange(W)[None, :]
    kb = kp[:, idx]
    vb = vp[:, idx]
    s = jnp.einsum('bnqrhe,bnkrhe->bnrhqk', qb.astype(jnp.float32),
                   kb.astype(jnp.float32)) * (E ** -0.5)
    rel = np.arange(W)[None, :] - n_side - np.arange(QBLK)[:, None]
    band = np.abs(rel) <= n_side
    jk = np.arange(nblk)[:, None] * QBLK - n_side + np.arange(W)[None, :]
    valid = (jk >= 0) & (jk < L)
    mask = band[None, :, :] & valid[:, None, :]
    dist = (np.abs(rel) * dilation).astype(np.float32)
    alibi = -slopes[:, None, None] * dist[None]
    s = s + alibi[None, None, None]
    s = jnp.where(mask[None, :, None, None], s, NEG)
    lse = jax.nn.logsumexp(s, axis=-1)
    p = jnp.exp(s - lse[..., None])
    o = jnp.einsum('bnrhqk,bnkrhe->bnqrhe', p, vb.astype(jnp.float32))
    o = o.reshape(B, Lp, dilation, H, E)[:, :L].reshape(B, S, H, E)
    lse = lse.transpose(0, 1, 4, 2, 3).reshape(B, Lp, dilation, H)[:, :L].reshape(B, S, H)
    return o, lse


def mixer_a(proj):
    B, S, _ = proj.shape
    qkv = proj.reshape(B, S, 3, A_N_GROUPS, A_HEADS_PER_GROUP, A_HEAD_DIM)
    slopes_all = jnp.asarray(alibi_slopes())
    outs, lses = [], []
    for g, (window, dilation) in enumerate(A_PATTERNS):
        n_side = (window // 2) // dilation
        sl = slopes_all[g * A_HEADS_PER_GROUP:(g + 1) * A_HEADS_PER_GROUP]
        o, l = dilated_window_attention(qkv[:, :, 0, g], qkv[:, :, 1, g],
                                        qkv[:, :, 2, g], dilation, n_side, sl)
        outs.append(o)
        lses.append(l)
    outs = jnp.stack(outs)
    wts = jax.nn.softmax(jnp.stack(lses), axis=0)
    comb = jnp.sum(wts[..., None] * outs, axis=0)
    return comb.reshape(B, S, A_OUT_W).astype(proj.dtype)


def mixer_b(proj_uv, v_norm_g, w_s, s_bias):
    B, S, _ = proj_uv.shape
    uv = jax.nn.gelu(proj_uv, approximate=False)
    u, v = uv[..., :B_W], uv[..., B_W:]
    v = rmsnorm(v, v_norm_g)
    vc = v.reshape(B, S // CHUNK, CHUNK, B_GROUPS, B_GROUP_W)
    mixed = jnp.einsum('gpq,bcqge->bcpge', w_s, vc) + s_bias.T[None, None, :, :, None]
    return u * mixed.reshape(B, S, B_W)


def memory_cross_attention(qm, mem_n, w_kv):
    B, S, _ = qm.shape
    M = mem_n.shape[1]
    kv = jnp.einsum('bmd,df->bmf', mem_n, w_kv).reshape(B, M, 2, MEM_HEADS, MEM_HEAD_DIM)
    q = qm.reshape(B, S, MEM_HEADS, MEM_HEAD_DIM).astype(jnp.float32)
    k = kv[:, :, 0].astype(jnp.float32)
    v = kv[:, :, 1].astype(jnp.float32)
    s = jnp.einsum('bshe,bmhe->bhsm', q, k) * (MEM_HEAD_DIM ** -0.5)
    p = jax.nn.softmax(s, axis=-1)
    o = jnp.einsum('bhsm,bmhe->bshe', p, v)
    return o.reshape(B, S, MEM_W).astype(qm.dtype)


def conv_ffn(h, w_up, conv_w, conv_b, w_down):
    a = jnp.einsum('bsd,df->bsf', h, w_up)
    ap = jnp.pad(a, ((0, 0), (1, 1), (0, 0)))
    a = ap[:, :-2] * conv_w[0] + ap[:, 1:-1] * conv_w[1] + ap[:, 2:] * conv_w[2] + conv_b
    gate, val = a[..., :FF], a[..., FF:]
    return jnp.einsum('bsf,fd->bsd', jax.nn.gelu(gate, approximate=False) * val, w_down)


def setup_inputs(seed: int = 0) -> dict:
    key = jax.random.key(seed)
    ks = jax.random.split(key, 20)

    def nrm(k, shape, scale):
        return jax.random.normal(k, shape, jnp.float32) * scale

    D = D_MODEL
    return {
        "x": nrm(ks[0], (BATCH, SEQ, D), 1.0),
        "mem": nrm(ks[1], (BATCH, MEM_LEN, D), 1.0),
        "mix_norm_g": 1.0 + nrm(ks[2], (DEPTH, D), 0.02),
        "ffn_norm_g": 1.0 + nrm(ks[3], (DEPTH, D), 0.02),
        "mem_norm_g": 1.0 + nrm(ks[4], (DEPTH, D), 0.02),
        "w_mem_kv": nrm(ks[5], (DEPTH, D, 2 * MEM_W), D ** -0.5),
        "a_w_in": nrm(ks[6], (N_A_LAYERS, D, A_IN), D ** -0.5),
        "a_w_out": nrm(ks[7], (N_A_LAYERS, A_OUT_IN, D), A_OUT_IN ** -0.5),
        "b_w_in": nrm(ks[8], (N_B_LAYERS, D, B_IN), D ** -0.5),
        "b_v_norm_g": 1.0 + nrm(ks[9], (N_B_LAYERS, B_W), 0.02),
        "b_w_s": nrm(ks[10], (N_B_LAYERS, B_GROUPS, CHUNK, CHUNK), CHUNK ** -0.5),
        "b_s_bias": 1.0 + nrm(ks[11], (N_B_LAYERS, B_GROUPS, CHUNK), 0.02),
        "b_w_out": nrm(ks[12], (N_B_LAYERS, B_OUT_IN, D), B_OUT_IN ** -0.5),
        "ffn_w_up": nrm(ks[13], (DEPTH, D, 2 * FF), D ** -0.5),
        "ffn_conv_w": nrm(ks[14], (DEPTH, CONV_W, 2 * FF), CONV_W ** -0.5),
        "ffn_conv_b": nrm(ks[15], (DEPTH, 2 * FF), 0.02),
        "ffn_w_down": nrm(ks[16], (DEPTH, FF, D), FF ** -0.5),
        "final_norm_g": 1.0 + nrm(ks[17], (D,), 0.02),
    }


def reference(x, mem, mix_norm_g, ffn_norm_g, mem_norm_g, w_mem_kv, a_w_in, a_w_out,
              b_w_in, b_v_norm_g, b_w_s, b_s_bias, b_w_out, ffn_w_up, ffn_conv_w,
              ffn_conv_b, ffn_w_down, final_norm_g):
    for i in range(DEPTH):
        h = rmsnorm(x, mix_norm_g[i])
        mem_n = rmsnorm(mem, mem_norm_g[i])
        j = i // N_MIXERS
        if i % N_MIXERS == 0:
            proj = jnp.einsum('bsd,df->bsf', h, a_w_in[j])
            tok = mixer_a(proj[..., :3 * A_QKV_W])
            mem_out = memory_cross_attention(proj[..., 3 * A_QKV_W:], mem_n, w_mem_kv[i])
            w_out = a_w_out[j]
        else:
            proj = jnp.einsum('bsd,df->bsf', h, b_w_in[j])
            tok = mixer_b(proj[..., :2 * B_W], b_v_norm_g[j], b_w_s[j], b_s_bias[j])
            mem_out = memory_cross_attention(proj[..., 2 * B_W:], mem_n, w_mem_kv[i])
            w_out = b_w_out[j]
        cat = jnp.concatenate([tok, mem_out], axis=-1)
        x = x + jnp.einsum('bsf,fd->bsd', cat, w_out)
        h = rmsnorm(x, ffn_norm_g[i])
        x = x + conv_ffn(h, ffn_w_up[i], ffn_conv_w[i], ffn_conv_b[i], ffn_w_down[i])
    return rmsnorm(x, final_norm_g)
```

```python
import os
import numpy as np
import concourse.bass as bass
import concourse.mybir as mybir
from concourse.bass_utils import run_bass_kernel_spmd

F32 = mybir.dt.float32
BF16 = mybir.dt.bfloat16
U8 = mybir.dt.uint8
AF = mybir.ActivationFunctionType
ALU = mybir.AluOpType

D = 2048
T = 1024
TE = T + 2
KD = 16
FF = 5632
NU = 88
EPS = 1e-6
SCALE = 128 ** -0.5
BIG = 1.0e6
N_LAYERS = 4
ARENA = 212000
PERS = 16384


class Buf:
    __slots__ = ("name", "lw", "rd", "rdd", "dsem", "dcount", "lo", "hi", "ov")

    def __init__(self, name, lo=None, hi=None):
        self.name = name
        self.lw = []
        self.rd = {}
        self.rdd = []
        self.dsem = None
        self.dcount = 0
        self.lo, self.hi = lo, hi
        self.ov = [self]


class Op:
    __slots__ = ("eng", "fn", "is_dma", "deps", "needs_inc", "inc_idx", "pos", "dsem", "dval", "dinc")

    def __init__(self, eng, fn, is_dma):
        self.eng, self.fn, self.is_dma = eng, fn, is_dma
        self.deps = []
        self.needs_inc = False
        self.inc_idx = 0
        self.pos = 0
        self.dsem = None
        self.dval = 0
        self.dinc = 16


class Rec:
    ENGS = ("pe", "act", "dve", "pool", "sp")

    def __init__(self, nc, stack):
        self.nc = nc
        self.stack = stack
        self.q = {e: [] for e in self.ENGS}
        self.esem = {e: stack.enter_context(nc.semaphore("es_" + e)) for e in self.ENGS}
        self.sbufs = []
        self.nsem = 0
        self.all_dma_bufs = []

    def sbuf(self, name, lo, hi):
        b = Buf(name, lo, hi)
        for o in self.sbufs:
            if o.lo < hi and lo < o.hi:
                o.ov.append(b)
                b.ov.append(o)
        self.sbufs.append(b)
        return b

    def _dsem(self, b):
        if b.dsem is None:
            b.dsem = self.stack.enter_context(self.nc.semaphore("ds%d" % self.nsem))
            self.nsem += 1
            self.all_dma_bufs.append(b)
        return b.dsem

    def _deps(self, op, reads, writes):
        cand = []
        for b in reads:
            for o in b.ov:
                cand += o.lw
        for b in writes:
            for o in b.ov:
                cand += o.lw
                cand += list(o.rd.values())
                cand += o.rdd
        seen = set()
        for p in cand:
            if p is op or id(p) in seen:
                continue
            seen.add(id(p))
            if (not p.is_dma) and p.eng == op.eng and not op.is_dma:
                if op.eng == "pe" or op.pos - p.pos > 3:
                    continue
            op.deps.append(p)
            if not p.is_dma:
                p.needs_inc = True
        for b in reads:
            if op.is_dma:
                b.rdd.append(op)
                if len(b.rdd) > 8:
                    b.rdd = b.rdd[-8:]
            else:
                b.rd[op.eng] = op
        for b in writes:
            b.lw = [op]
            b.rd = {}
            b.rdd = []

    def compute(self, eng, fn, reads=(), writes=()):
        op = Op(eng, fn, False)
        op.pos = len(self.q[eng])
        self._deps(op, reads, writes)
        self.q[eng].append(op)
        return op

    def dma(self, q, out_ap, in_ap, reads, writes, sem_buf):
        op = Op(q, (lambda e, o=out_ap, i=in_ap: e.dma_start(out=o, in_=i)), True)
        op.pos = len(self.q[q])
        op.dsem = self._dsem(sem_buf)
        sem_buf.dcount += 16
        op.dval = sem_buf.dcount
        self._deps(op, reads, writes)
        self.q[q].append(op)
        return op

    def collective(self, in_ap, out_ap, reads, writes, sem_buf, groups):
        def fn(e, i=in_ap, o=out_ap):
            return e.collective_compute("AllGather", ALU.bypass, replica_groups=groups, ins=[i], outs=[o])
        op = Op("pool", fn, True)
        op.dinc = 1
        op.pos = len(self.q["pool"])
        op.dsem = self._dsem(sem_buf)
        sem_buf.dcount += 1
        op.dval = sem_buf.dcount
        self._deps(op, reads, writes)
        self.q["pool"].append(op)
        return op

    def finish(self):
        op = Op("sp", None, True)
        for b in self.all_dma_bufs:
            fake = Op("sp", None, True)
            fake.dsem, fake.dval = b.dsem, b.dcount
            op.deps.append(fake)
        self.q["sp"].append(op)

    def finalize(self):
        for e in self.ENGS:
            c = 0
            for op in self.q[e]:
                if (not op.is_dma) and op.needs_inc:
                    c += 1
                    op.inc_idx = c

    def emit(self, eng, e):
        obs = {}
        for op in self.q[eng]:
            for p in op.deps:
                if p.is_dma:
                    key, val = p.dsem, p.dval
                else:
                    key, val = self.esem[p.eng], p.inc_idx
                if obs.get(key, 0) < val:
                    e.wait_ge(key, val)
                    obs[key] = val
            if op.fn is None:
                continue
            ins = op.fn(e)
            if op.is_dma:
                ins.then_inc(op.dsem, op.dinc)
            elif op.needs_inc:
                ins.then_inc(self.esem[eng], 1)


class _Stop(Exception):
    pass


class Builder:
    def __init__(self, layers, final_norm, n_a_su=12, n_b_su=9):
        import contextlib
        self.layers = layers
        self.stack = contextlib.ExitStack()
        nc = self.nc = bass.Bass("TRN2", target_bir_lowering=False)
        self.rec = Rec(nc, self.stack)
        self.arena = nc.alloc_sbuf_tensor("arena", [128, ARENA], U8)
        self.psum = nc.alloc_psum_tensor("ps", [128, 4096], F32)
        self.bank = [Buf("bank%d" % i) for i in range(8)]
        self.dram = {}
        self._sbc = {}
        self.final_norm = final_norm

    def V(self, off, shape, dt):
        sz = 4 if dt == F32 else 2
        n = int(np.prod(shape[1:]))
        ap = self.arena[:, off:off + n * sz].bitcast(dt)
        if len(shape) == 3:
            ap = ap.rearrange("p (a b) -> p a b", b=shape[2])
        elif len(shape) == 4:
            ap = ap.rearrange("p (a b c) -> p a b c", b=shape[2], c=shape[3])
        return ap

    def SB(self, name, off, shape, dt):
        sz = 4 if dt == F32 else 2
        n = int(np.prod(shape[1:]))
        key = (name, off, n * sz)
        if key not in self._sbc:
            assert off + n * sz <= ARENA, (name, off, n * sz)
            self._sbc[key] = self.rec.sbuf(name, off, off + n * sz)
        return self.V(off, shape, dt), self._sbc[key]

    def PS(self, b, n=512, off=0):
        return self.psum[:, b * 512 + off: b * 512 + off + n]

    def din(self, name, shape, dt=F32):
        t = self.nc.dram_tensor(name, list(shape), dt, kind="ExternalInput")
        self.dram[name] = t
        return t

    def mm(self, out, lhsT, rhs, start, stop, reads, writes, skip=False):
        self.rec.compute("pe", (lambda e, o=out, l=lhsT, r=rhs, s=start, st=stop, sk=skip:
                                e.matmul(o, l, r, start=s, stop=st, skip_group_check=sk)), reads, writes)

    def act(self, out, in_, func, reads, writes, scale=1.0, bias=0.0, accum=None):
        if accum is None:
            fn = (lambda e, o=out, i=in_, f=func, s=scale, b=bias: e.activation(o, i, f, bias=b, scale=s))
        else:
            fn = (lambda e, o=out, i=in_, f=func, s=scale, b=bias, a=accum:
                  e.activation(o, i, f, bias=b, scale=s, accum_out=a))
        self.rec.compute("act", fn, reads, writes)

    def dve(self, fn, reads, writes):
        self.rec.compute("dve", fn, reads, writes)

    def build(self):
        nc, rec = self.nc, self.rec
        L = self.layers
        x_in = self.din("xT", [KD, 128, T])
        memT = self.din("memT", [KD, 128, 256])
        gains_d = self.din("gains", [128, 13 * 16])
        convp_d = self.din("convp", [128, 4 * NU * 4])
        dt01_d = self.din("dt01", [128, 2 * 3 * 384])
        dt2_d = self.din("dt2", [128, 64])
        hm_d = self.din("hm", [128, 2])
        win_d, wout_d, wup_d, wdn_d, bc_d = {}, {}, {}, {}, {}
        for l in L:
            isA = (l % 2 == 0)
            win_d[l] = self.din("win%d" % l, [12 if isA else 9, 128, KD, 512])
            wout_d[l] = self.din("wout%d" % l, [16, 128, 8 if isA else 16, 128])
            wup_d[l] = self.din("wup%d" % l, [NU, 128, KD, 128])
            wdn_d[l] = self.din("wdn%d" % l, [16, 128, 44, 128])
            if not isA:
                bc_d[l] = (self.din("gvb%d" % l, [128, 1536]), self.din("sbb%d" % l, [128, 1536]),
                           self.din("wst%d" % l, [128, 1536]))
        if self.final_norm:
            y_out = nc.dram_tensor("yT", [KD, 128, T], F32, kind="ExternalOutput")
        else:
            y_out = nc.dram_tensor("xo", [KD, 128, T], F32, kind="ExternalOutput")
        xd = nc.dram_tensor("xd", [KD, 128, T], F32)
        kloc = [nc.dram_tensor("kloc%d" % g, [512, T], BF16) for g in range(3)]
        vloc = [nc.dram_tensor("vloc%d" % g, [T, 512], BF16) for g in range(3)]
        kall = [nc.dram_tensor("kall%d" % g, [1024, T], BF16) for g in range(3)]
        vall = [nc.dram_tensor("vall%d" % g, [2 * T, 512], BF16) for g in range(3)]
        hbloc = nc.dram_tensor("hbloc", [128, 32], BF16)
        hball = nc.dram_tensor("hball", [256, 32], BF16)
        self.xd_b = [Buf("xd%d" % k) for k in range(KD)]
        self.xin_b = [Buf("xin%d" % k) for k in range(KD)]
        klocb = [Buf("kloc%d" % i) for i in range(3)]
        vlocb = [Buf("vloc%d" % i) for i in range(3)]
        kallb = [Buf("kall%d" % g) for g in range(3)]
        vallb = [Buf("vall%d" % g) for g in range(3)]
        hblocb, hballb = Buf("hbloc"), Buf("hball")
        cck, ccv, cch = [Buf("cck%d" % g) for g in range(3)], [Buf("ccv%d" % g) for g in range(3)], Buf("cch")
        GROUPS = [[0, 1], [2, 3], [4, 5], [6, 7]]

        ones, ones_b = self.SB("ones", 0, [128, 128], BF16)
        gains, gains_b = self.SB("gains", 256, [128, 208], F32)
        convp, convp_b = self.SB("convp", 1088, [128, 4, NU, 4], F32)
        dt01, dt01_b = self.SB("dt01", 6720, [128, 6, 384], F32)
        dt2, dt2_b = self.SB("dt2", 15936, [128, 64], F32)
        hm, hm_b = self.SB("hm", 16192, [128, 2], F32)
        rec.compute("dve", lambda e: e.memset(ones, 1.0), (), (ones_b,))
        EPSB, epsb_b = self.SB("epsb", 16224, [128, 1], F32)
        rec.compute("dve", lambda e: e.memset(EPSB, EPS), (), (epsb_b,))
        rec.dma("sp", gains, gains_d[:, :], (), (gains_b,), gains_b)
        rec.dma("sp", self.V(1088, [128, 4 * NU * 4], F32), convp_d[:, :], (), (convp_b,), convp_b)
        rec.dma("sp", self.V(6720, [128, 2304], F32), dt01_d[:, :], (), (dt01_b,), dt01_b)
        rec.dma("sp", dt2, dt2_d[:, :], (), (dt2_b,), dt2_b)
        rec.dma("sp", hm, hm_d[:, :], (), (hm_b,), hm_b)
        self.ones, self.ones_b = ones, ones_b
        self.gains, self.gains_b = gains, gains_b

        B0 = PERS
        HT, HT_b = self.SB("HT", B0, [128, KD, TE], BF16)
        HTK = [self.rec.sbuf("HTk%d" % k, B0 + 2052 * k, B0 + 2052 * (k + 1)) for k in range(KD)]
        POOL_K = ()
        NRM = ARENA - 8192
        SQ = [self.SB("sq%d" % i, NRM + 2048 * i, [128, T], BF16) for i in range(2)]
        RSTD, RSTD_b = self.SB("rstd", NRM + 4096, [128, T], F32)
        XSo = B0 + 32832
        XS = [self.SB("xs%d" % k, XSo + 4096 * k, [128, T], F32) for k in range(KD)]

        XSM = [self.SB("xsm%d" % k, B0 + 81984 + 4096 * k, [128, T], F32) for k in range(8)] + \
              [XS[k] for k in range(8, KD)]

        XSF = [self.SB("xsf%d" % k, B0 + 98368 + 4096 * k, [128, T], F32) for k in range(KD)]

        def norm_load(src, src_b, k, xs=None):
            xs = XS if xs is None else xs
            rec.dma("sp", xs[k][0], src[k, :, :], (src_b[k],), (xs[k][1],), xs[k][1])

        def norm_stats(k, xs=None):
            xs = XS if xs is None else xs
            sq, sq_b = SQ[k % 2]
            self.act(sq, xs[k][0], AF.Square, (xs[k][1],), (sq_b,))
            for hf in range(2):
                self.mm(self.PS(hf), ones, sq[:, hf * 512:(hf + 1) * 512], k == 0, k == KD - 1,
                        (ones_b, sq_b), (self.bank[hf],))

        def norm_stats_from(k, x_ap, x_b):
            sq, sq_b = SQ[k % 2]
            self.act(sq, x_ap, AF.Square, (x_b,), (sq_b,))

            def pe_part(k=k, sq=sq, sq_b=sq_b):
                for hf in range(2):
                    self.mm(self.PS(hf), ones, sq[:, hf * 512:(hf + 1) * 512], k == 0, k == KD - 1,
                            (ones_b, sq_b), (self.bank[hf],))
            return pe_part

        def norm(src, src_b, gcol, out_fn, early=None, preloaded=False, sources=None):
            if sources is None:
                sources = [XSF[k] for k in range(KD)]
            if not preloaded:
                for k in range(KD):
                    norm_load(src, src_b, k, XSF)
                for k in range(KD):
                    norm_stats(k, XSF)
            tmp = self.psum[:, 0:1024]
            self.act(RSTD, tmp, AF.Ln, (self.bank[0], self.bank[1]), (RSTD_b,), scale=1.0 / D, bias=EPSB)
            self.act(RSTD, RSTD, AF.Exp, (RSTD_b,), (RSTD_b,), scale=-0.5)
            if early is not None:
                early(sources)
            for k in [12, 13, 14, 15] + list(range(12)):
                out_fn(k, sources[k][0], sources[k][1], gains[:, gcol * 16 + k: gcol * 16 + k + 1])

        def norm_to_HT(src, src_b, gcol, early=None, preloaded=False, sources=None):
            def o(k, xs, xs_b, g):
                eng = "pool" if k in POOL_K else "dve"
                rec.compute(eng, lambda e, k=k, xs=xs, g=g: e.scalar_tensor_tensor(
                    out=HT[:, k, 1:1 + T], in0=xs, scalar=g, in1=RSTD, op0=ALU.mult, op1=ALU.mult),
                    (xs_b, RSTD_b, gains_b), (HTK[k],))
            norm(src, src_b, gcol, o, early, preloaded, sources)

        def proj_residual(nk, rhs, rhs_b, w_d, wslots, xc, src, src_b, dst, dst_b, bank_sets,
                          stats=False, prefetch=False, preloaded=0, xs=None):
            nw = len(wslots)
            nx = len(xc)
            assert nx == 4
            for m in range(preloaded, min(nw, 16)):
                rec.dma("pool", wslots[m][0], w_d[m, :, :, :], (), (wslots[m][1],), wslots[m][1])

            def loadx(m):
                x, x_b = xc[m % nx]
                rec.dma("sp", x, src[m, :, :], (src_b[m],), (x_b,), x_b)
            for m in range(nx - 2):
                loadx(m)
            pend = []
            for m in range(16):
                if m + nx - 2 < 16:
                    loadx(m + nx - 2)
                if prefetch and m >= 2 and m - 2 < 12:
                    norm_load(dst, dst_b, m - 2)
                while pend and pend[0][0] <= m - 2:
                    pend.pop(0)[1]()
                w, w_b = wslots[m % nw]
                bs = bank_sets[m % len(bank_sets)]
                x, x_b = xc[m % nx]
                for k in range(nk):
                    for hf in range(2):
                        self.mm(self.PS(bs[hf]), w[:, k, :], rhs(k, hf), k == 0, k == nk - 1,
                                (w_b, rhs_b(k) if callable(rhs_b) else rhs_b), (self.bank[bs[hf]],))
                if m + nw < 16:
                    rec.dma("pool", w, w_d[m + nw, :, :, :], (), (w_b,), w_b)
                for hf in range(2):
                    self.dve(lambda e, x=x, hf=hf, b=bs[hf]: e.tensor_tensor(
                        out=x[:, hf * 512:(hf + 1) * 512], in0=self.PS(b), in1=x[:, hf * 512:(hf + 1) * 512],
                        op=ALU.add), (self.bank[bs[hf]], x_b), (x_b,))
                if stats:
                    pend.append((m, norm_stats_from(m, x, x_b)))
                rec.dma("sp", dst[m, :, :], x, (x_b,), (dst_b[m],), x_b)
            while pend:
                pend.pop(0)[1]()
            xs = XS if xs is None else xs
            if stats and not prefetch:
                for k in range(12):
                    norm_load(dst, dst_b, k, xs)
            return [xs[k] for k in range(12)] + [xc[k % nx] for k in range(12, 16)]

        def mem_prepare(l, w_d, su0, WIN, MEMN, KMT, VM, XCm):
            memn, memn_b = MEMN
            ms, ms_b = self.SB("memstage", B0, [128, KD, 256], F32)
            rec.dma("sp", ms, memT.ap().rearrange("k p t -> p k t"), (), (ms_b,), ms_b)
            for k in range(KD):
                sq, sq_b = SQ[k % 2]
                self.act(sq[:, 0:256], ms[:, k, :], AF.Square, (ms_b,), (sq_b,))
                self.mm(self.PS(6, 256), ones, sq[:, 0:256], k == 0, k == KD - 1, (ones_b, sq_b), (self.bank[6],))
            rs = RSTD[:, 0:256]
            self.act(rs, self.PS(6, 256), AF.Ln, (self.bank[6],), (RSTD_b,), scale=1.0 / D, bias=EPSB)
            self.act(rs, rs, AF.Exp, (RSTD_b,), (RSTD_b,), scale=-0.5)
            for k in range(KD):
                g = gains[:, (8 + l) * 16 + k: (8 + l) * 16 + k + 1]
                self.dve(lambda e, k=k, g=g: e.scalar_tensor_tensor(
                    out=memn[:, k, :], in0=ms[:, k, :], scalar=g, in1=rs, op0=ALU.mult, op1=ALU.mult),
                    (ms_b, RSTD_b, gains_b), (memn_b,))
            kmt, kmt_b = KMT
            vm, vm_b = VM
            w, w_b = WIN[su0 % 2]
            for hh in range(4):
                for k in range(KD):
                    self.mm(self.PS(6, 256), w[:, k, hh * 128:(hh + 1) * 128], memn[:, k, :], k == 0, k == KD - 1,
                            (w_b, memn_b), (self.bank[6],))
                self.act(kmt[:, hh, :], self.PS(6, 256), AF.Copy, (self.bank[6],), (kmt_b,))
            w, w_b = WIN[(su0 + 1) % 2]
            for tt in range(2):
                for k in range(KD):
                    self.mm(self.PS(7), memn[:, k, tt * 128:(tt + 1) * 128], w[:, k, :], k == 0, k == KD - 1,
                            (w_b, memn_b), (self.bank[7],))
                self.act(vm[:, tt, :], self.PS(7), AF.Copy, (self.bank[7],), (vm_b,))

        def mem_attention(QMT, KMT, VM, CAT, c0, PT, RC):
            qmt, qmt_b = QMT
            kmt, kmt_b = KMT
            vm, vm_b = VM
            cat, cat_b = CAT
            rc, rc_b = RC
            for hh in range(4):
                items = [(hf, kt) for hf in range(2) for kt in range(2)]
                sb_of = lambda i: 4 + (i % 3)

                def s_mm(i):
                    hf, kt = items[i]
                    self.mm(self.PS(sb_of(i)), kmt[:, hh, kt * 128:(kt + 1) * 128],
                            qmt[:, hh, hf * 512:(hf + 1) * 512], True, True, (kmt_b, qmt_b), (self.bank[sb_of(i)],))
                s_mm(0)
                for i in range(4):
                    if i + 1 < 4:
                        s_mm(i + 1)
                    hf, kt = items[i]
                    pt, pt_b = PT[i % len(PT)]
                    self.act(pt, self.PS(sb_of(i)), AF.Exp, (self.bank[sb_of(i)],), (pt_b,), scale=SCALE)
                    self.mm(self.PS(hf), vm[:, kt, hh * 128:(hh + 1) * 128], pt, kt == 0, kt == 1,
                            (vm_b, pt_b), (self.bank[hf],))
                    self.mm(self.PS(2 + hf), ones, pt, kt == 0, kt == 1, (ones_b, pt_b), (self.bank[2 + hf],))
                self.act(rc, self.psum[:, 1024:2048], AF.Ln, (self.bank[2], self.bank[3]), (rc_b,))
                self.act(rc, rc, AF.Exp, (rc_b,), (rc_b,), scale=-1.0)
                self.dve(lambda e, hh=hh: e.tensor_tensor(out=cat[:, c0 + hh, :], in0=self.psum[:, 0:1024], in1=rc,
                                                          op=ALU.mult), (self.bank[0], self.bank[1], rc_b), (cat_b,))

        def su_fm(w, w_b, evac):
            for sub in range(4):
                bs = (4, 5) if sub % 2 == 0 else (6, 7)
                for k in range(KD):
                    for hf in range(2):
                        self.mm(self.PS(bs[hf]), w[:, k, sub * 128:(sub + 1) * 128],
                                HT[:, k, 1 + hf * 512: 1 + (hf + 1) * 512], k == 0, k == KD - 1,
                                (w_b, HT_b), (self.bank[bs[hf]],))
                evac(sub, self.psum[:, bs[0] * 512: bs[0] * 512 + 1024], (self.bank[bs[0]], self.bank[bs[1]]))

        def su_tm(w, w_b, evac):
            for tt in range(8):
                b = tt % 4
                for k in range(KD):
                    self.mm(self.PS(b), HT[:, k, 1 + tt * 128: 1 + (tt + 1) * 128], w[:, k, :], k == 0, k == KD - 1,
                            (w_b, HT_b), (self.bank[b],))
                evac(tt, self.PS(b), (self.bank[b],))

        cur, cur_b = x_in, self.xin_b
        msrc = None
        stop_at = int(os.environ.get("KSTOP", "99"))

        def chk(n):
            if n >= stop_at:
                raise _Stop()
        try:
          for l in L:
              isA = (l % 2 == 0)
              j = l // 2
              nsu = 12 if isA else 9
              wd = win_d[l]
              if msrc is None:
                  norm_to_HT(cur, cur_b, l)
              else:
                  norm_to_HT(cur, cur_b, l, preloaded=True, sources=msrc)
              chk(1)
              WIN = [self.SB("win%d" % i, B0 + 32832 + 16384 * i, [128, KD, 512], BF16) for i in range(2)]
              su_next = [0]

              def su_load(s):
                  w, w_b = WIN[s % 2]
                  rec.dma("pool", w, wd[s, :, :, :], (), (w_b,), w_b)

              su_load(0)
              su_load(1)

              def su_done(s):
                  if s + 2 < nsu:
                      su_load(s + 2)

              XCm = [self.SB("xc%d" % i, ARENA - 8192 - 8192 + 4096 * i, [128, T], F32) for i in range(2)]
              if isA:
                  STG = [self.SB("stg%d" % i, B0 + 65600 + 8192 * i, [128, 4096], BF16) for i in range(4)]
                  QT = self.SB("QT", B0 + 98368, [128, 12, T], BF16)
                  QMT = self.SB("QMT", B0 + 122944, [128, 4, T], BF16)
                  CAT = self.SB("CAT", B0 + 131136, [128, 8, T], BF16)
                  MEMN = self.SB("MEMN", B0 + 163904, [128, KD, 256], BF16)
                  KMT = self.SB("KMT", B0 + 172096, [128, 4, 256], BF16)
                  VM = self.SB("VM", B0 + 174144, [128, 2, 512], BF16)
                  WK = B0 + 86016
                  SBW = [self.SB("sbw%d" % i, WK + 1536 * i, [128, 384], F32) for i in range(3)]
                  PT = [self.SB("pt%d" % i, WK + 4608 + 1024 * i, [128, 512], BF16) for i in range(3)]
                  PTA = [self.SB("pta%d" % i, WK + 4608 + 1024 * i, [128, 384], BF16) for i in range(3)]
                  DS = [self.SB("dsum%d" % i, WK + 4608 + 1024 * i + 768, [128, 128], BF16) for i in range(3)]
                  RC = self.SB("rc", WK + 7680, [128, T], F32)
                  for s in range(3):
                      w, w_b = WIN[s % 2]
                      st, st_b = STG[s % 2]
                      st3 = st.rearrange("p (h t) -> p h t", t=T)

                      def ev(sub, ps, pb, st3=st3, st_b=st_b):
                          self.act(st3[:, sub, :], ps, AF.Copy, pb, (st_b,))
                      su_fm(w, w_b, ev)
                      su_done(s)
                      rec.dma("sp", kloc[s][:, :].rearrange("(h e) t -> e h t", e=128), st3,
                              (st_b,), (klocb[s],), st_b)
                      rec.collective(kloc[s].ap().opt(), kall[s].ap().opt(), (klocb[s],), (kallb[s],), cck[s], GROUPS)
                  chk(2)
                  for s in range(3, 6):
                      w, w_b = WIN[s % 2]
                      st, st_b = STG[2 + s % 2]
                      st3 = st.rearrange("p (t c) -> p t c", c=512)

                      def ev(tt, ps, pb, st3=st3, st_b=st_b):
                          self.act(st3[:, tt, :], ps, AF.Copy, pb, (st_b,))
                      su_tm(w, w_b, ev)
                      su_done(s)
                      g = s - 3
                      rec.dma("sp", vloc[g][:, :].rearrange("(t p) c -> p t c", p=128), st3,
                              (st_b,), (vlocb[g],), st_b)
                      rec.collective(vloc[g].ap().opt(), vall[g].ap().opt(), (vlocb[g],), (vallb[g],), ccv[g], GROUPS)
                  chk(3)
                  qt, qt_b = QT
                  for s in range(6, 9):
                      w, w_b = WIN[s % 2]

                      def ev(sub, ps, pb, g=s - 6):
                          self.act(qt[:, g * 4 + sub, :], ps, AF.Copy, pb, (qt_b,))
                      su_fm(w, w_b, ev)
                      su_done(s)
                  qmt, qmt_b = QMT
                  w, w_b = WIN[9 % 2]

                  def ev(sub, ps, pb):
                      self.act(qmt[:, sub, :], ps, AF.Copy, pb, (qmt_b,))
                  su_fm(w, w_b, ev)
                  su_done(9)
                  mem_prepare(l, wd, 10, WIN, MEMN, KMT, VM, XCm)
                  mem_attention(QMT, KMT, VM, CAT, 4, PT, RC)
                  chk(4)
                  K0 = self.SB("K0", B0 + 0, [128, 4, 1280], BF16)
                  K1 = self.SB("K1", B0 + 10240, [128, 4, 2048], BF16)
                  K2 = self.SB("K2", B0 + 26624, [128, 4, 2048], BF16)
                  V0 = self.SB("V0", B0 + 43008, [128, 10, 512], BF16)
                  V1 = self.SB("V1", B0 + 53248, [128, 4, 4, 512], BF16)
                  V2 = self.SB("V2", B0 + 69632, [128, 16, 512], BF16)
                  kl = lambda g, c0, c1: kloc[g][:, c0:c1].rearrange("(h e) t -> e h t", e=128)
                  ka = lambda R, g, c0, c1: kall[g][R * 512:(R + 1) * 512, c0:c1].rearrange("(h e) t -> e h t", e=128)
                  k0, k0b = K0
                  rec.dma("sp", k0[:, :, 128:1152], kl(0, 0, T), (klocb[0],), (k0b,), k0b)
                  rec.dma("sp", k0[:, :, 0:128], ka(0, 0, 896, 1024), (kallb[0],), (k0b,), k0b)
                  rec.dma("sp", k0[:, :, 1152:1280], ka(1, 0, 0, 128), (kallb[0],), (k0b,), k0b)
                  k1, k1b = K1
                  rec.dma("sp", k1[:, :, 512:1536], kl(1, 0, T), (klocb[1],), (k1b,), k1b)
                  rec.dma("sp", k1[:, :, 0:512], ka(0, 1, 512, 1024), (kallb[1],), (k1b,), k1b)
                  rec.dma("sp", k1[:, :, 1536:2048], ka(1, 1, 0, 512), (kallb[1],), (k1b,), k1b)
                  k2, k2b = K2
                  rec.dma("sp", k2[:, :, 0:1024], ka(0, 2, 0, T), (kallb[2],), (k2b,), k2b)
                  rec.dma("sp", k2[:, :, 1024:2048], ka(1, 2, 0, T), (kallb[2],), (k2b,), k2b)
                  v0, v0b = V0
                  rec.dma("sp", v0[:, 1:9, :], vloc[0][:, :].rearrange("(t p) c -> p t c", p=128), (vlocb[0],), (v0b,), v0b)
                  rec.dma("sp", v0[:, 0, :], vall[0][896:1024, :], (vallb[0],), (v0b,), v0b)
                  rec.dma("sp", v0[:, 9, :], vall[0][1024:1152, :], (vallb[0],), (v0b,), v0b)
                  v1, v1b = V1
                  for a_ in range(2):
                      rec.dma("sp", v1[:, 1 + a_, :, :],
                              vloc[1][a_ * 512:(a_ + 1) * 512, :].rearrange("(p r) c -> p r c", r=4),
                              (vlocb[1],), (v1b,), v1b)
                  rec.dma("sp", v1[:, 0, :, :], vall[1][512:1024, :].rearrange("(p r) c -> p r c", r=4),
                          (vallb[1],), (v1b,), v1b)
                  rec.dma("sp", v1[:, 3, :, :], vall[1][1024:1536, :].rearrange("(p r) c -> p r c", r=4),
                          (vallb[1],), (v1b,), v1b)
                  v2, v2b = V2
                  rec.dma("sp", v2[0:64, :, :], vall[2][0:1024, :].rearrange("(p r) c -> p r c", r=16),
                          (vallb[2],), (v2b,), v2b)
                  rec.dma("sp", v2[64:128, :, :], vall[2][1024:2048, :].rearrange("(p r) c -> p r c", r=16),
                          (vallb[2],), (v2b,), v2b)
                  cat, cat_b = CAT
                  rc, rc_b = RC
                  slopes = [2.0 ** (-8.0 * (i + 1) / 12) for i in range(12)]
                  chk(5)
                  for hh in range(4):
                      blocks = []
                      for b in range(8):
                          v = 0 if b == 0 else (2 if b == 7 else 1)
                          sm = [(k0[:, hh, (b + di) * 128:(b + di + 1) * 128], qt[:, hh, b * 128:(b + 1) * 128], di * 128, 128)
                                for di in range(3)]
                          pv = [(b // 4, (b % 4) * 128, 128, 1, v0[:, b + di, hh * 128:(hh + 1) * 128], di * 128, 128)
                                for di in range(3)]
                          blocks.append((sm, dt01[:, v, :], 384, pv, slopes[hh], (k0b,), (v0b,)))
                      for r in range(4):
                          for b in range(2):
                              v = 3 + (0 if b == 0 else 2)
                              sm = [(k1[:, hh, 512 * (b + di) + r: 512 * (b + di) + r + 509: 4],
                                     qt[:, 4 + hh, 512 * b + r: 512 * b + r + 509: 4], di * 128, 128) for di in range(3)]
                              pv = [(b, r, 128, 4, v1[:, b + di, r, hh * 128:(hh + 1) * 128], di * 128, 128)
                                    for di in range(3)]
                              blocks.append((sm, dt01[:, v, :], 384, pv, slopes[4 + hh], (k1b,), (v1b,)))
                      for r in range(16):
                          sm = [(k2[:, hh, r: r + 2033: 16], qt[:, 8 + hh, r: r + 1009: 16], 0, 64)]
                          pv = [(0, r, 32, 16, v2[:, r, hh * 128:(hh + 1) * 128], 0, 32),
                                (1, r, 32, 16, v2[:, r, hh * 128:(hh + 1) * 128], 32, 32)]
                          blocks.append((sm, dt2[:, :], 64, pv, slopes[8 + hh], (k2b,), (v2b,)))
                      touched = set()

                      def s_stage(i):
                          sm, dist, nc_, pv, slope, kb, vb = blocks[i]
                          bk = 4 + (i % 3)
                          for (lhsT, rhs, co, n) in sm:
                              self.mm(self.PS(bk, n, co), lhsT, rhs, True, True, kb + (qt_b,), (self.bank[bk],))
                      s_stage(0)
                      s_stage(1)
                      for i in range(len(blocks)):
                          if i + 2 < len(blocks):
                              s_stage(i + 2)
                          sm, dist, nc_, pv, slope, kb, vb = blocks[i]
                          bk = 4 + (i % 3)
                          sbw, sbw_b = SBW[i % 3]
                          pt, pt_b = PTA[i % 3]
                          ds, ds_b = DS[i % 3]
                          self.dve(lambda e, sbw=sbw, dist=dist, nc_=nc_, bk=bk, slope=slope: e.scalar_tensor_tensor(
                              out=sbw[:, 0:nc_], in0=dist, scalar=-slope / SCALE, in1=self.PS(bk, nc_),
                              op0=ALU.mult, op1=ALU.add), (self.bank[bk], dt01_b, dt2_b), (sbw_b,))
                          self.act(pt[:, 0:nc_], sbw[:, 0:nc_], AF.Exp, (sbw_b,), (pt_b,), scale=SCALE)
                          fold = (nc_ == 384)
                          if fold:
                              self.dve(lambda e, ds=ds, pt=pt: e.tensor_tensor(out=ds, in0=pt[:, 0:128], in1=pt[:, 128:256],
                                                                               op=ALU.add), (pt_b,), (ds_b,))
                              self.dve(lambda e, ds=ds, pt=pt: e.tensor_tensor(out=ds, in0=ds, in1=pt[:, 256:384],
                                                                               op=ALU.add), (pt_b, ds_b), (ds_b,))
                          for pvi, (ob, c0, n, step, lhsT, pc, pn) in enumerate(pv):
                              if step == 1:
                                  o_ap = self.PS(ob, n, c0)
                                  d_ap = self.PS(2 + ob, n, c0)
                              else:
                                  o_ap = self.psum[:, ob * 512 + c0: ob * 512 + c0 + (n - 1) * step + 1: step]
                                  d_ap = self.psum[:, (2 + ob) * 512 + c0: (2 + ob) * 512 + c0 + (n - 1) * step + 1: step]
                              first = ob not in touched
                              touched.add(ob)
                              if pvi == 0:
                                  first_den = first
                              self.mm(o_ap, lhsT, pt[:, pc:pc + pn], first, False, vb + (pt_b,), (self.bank[ob],), skip=True)
                              if not fold:
                                  self.mm(d_ap, ones, pt[:, pc:pc + pn], first, False, (ones_b, pt_b), (self.bank[2 + ob],),
                                          skip=True)
                              elif pvi == 2:
                                  self.mm(d_ap, ones, ds, first_den, False, (ones_b, ds_b), (self.bank[2 + ob],), skip=True)
                      self.act(rc, self.psum[:, 1024:2048], AF.Ln, (self.bank[2], self.bank[3]), (rc_b,))
                      self.act(rc, rc, AF.Exp, (rc_b,), (rc_b,), scale=-1.0)
                      self.dve(lambda e, hh=hh, cat=cat, rc=rc: e.tensor_tensor(
                          out=cat[:, hh, :], in0=self.psum[:, 0:1024], in1=rc, op=ALU.mult),
                          (self.bank[0], self.bank[1], rc_b), (cat_b,))
                  nk_out = 8
                  chk(6)
              else:
                  gvb_d, sbb_d, wst_d = bc_d[l]
                  VG = self.SB("VG", B0 + 65600, [128, 8, 1536], F32)
                  UT = self.SB("UT", B0 + 65600, [128, 12, T], F32)
                  VN = self.SB("VN", B0 + 114752, [128, 8, 1536], BF16)
                  QMT = self.SB("QMTb", B0 + 139328, [128, 4, T], BF16)
                  MEMN = self.SB("MEMNb", B0 + 147520, [128, KD, 256], BF16)
                  KMT = self.SB("KMTb", B0 + 155712, [128, 4, 256], BF16)
                  VM = self.SB("VMb", B0 + 157760, [128, 2, 512], BF16)
                  GVB = self.SB("GVB", B0 + 159808, [128, 1536], F32)
                  SBB = self.SB("SBB", B0 + 165952, [128, 12, 128], F32)
                  WST = self.SB("WST", B0 + 172096, [128, 12, 128], BF16)
                  WK = B0 + 175168
                  TMPB = [self.SB("tmpb%d" % i, WK + 2048 * i, [128, 512], F32) for i in range(2)]
                  PT = [self.SB("ptb%d" % i, WK + 4096 + 1024 * i, [128, 512], BF16) for i in range(3)]
                  RC = self.SB("rcb", WK + 7168, [128, T], F32)
                  SS = self.SB("ssb", WK + 11264, [128, 8], F32)
                  RS8 = self.SB("rs8", WK + 11296, [128, 8], F32)
                  CAT = self.SB("CATb", B0 + 0, [128, 16, T], BF16)
                  rec.dma("sp", GVB[0], gvb_d[:, :], (), (GVB[1],), GVB[1])
                  rec.dma("sp", self.V(B0 + 165952, [128, 1536], F32), sbb_d[:, :], (), (SBB[1],), SBB[1])
                  vg, vg_b = VG
                  vn, vn_b = VN
                  ss, ss_b = SS
                  rs8, rs8_b = RS8
                  self.dve(lambda e, ss=ss: e.memset(ss, 0.0), (), (ss_b,))
                  junk, junk_b = self.SB("junk", B0 + 139328, [128, 1536], BF16)
                  for s in range(3):
                      w, w_b = WIN[s % 2]

                      def ev(tt, ps, pb, s=s):
                          self.act(vg[:, tt, s * 512:(s + 1) * 512], ps, AF.Gelu, pb, (vg_b,))
                          if s == 2:
                              self.act(junk, vg[:, tt, :], AF.Square, (vg_b,), (junk_b, ss_b), accum=ss[:, tt:tt + 1])
                              self.act(rs8[:, tt:tt + 1], ss[:, tt:tt + 1], AF.Ln, (ss_b,), (rs8_b,), scale=1.0 / 1536,
                                       bias=EPSB)
                              self.act(rs8[:, tt:tt + 1], rs8[:, tt:tt + 1], AF.Exp, (rs8_b,), (rs8_b,), scale=-0.5)
                              self.dve(lambda e, tt=tt, vn=vn, vg=vg, rs8=rs8, GVB=GVB: e.scalar_tensor_tensor(
                                  out=vn[:, tt, :], in0=vg[:, tt, :], scalar=rs8[:, tt:tt + 1], in1=GVB[0],
                                  op0=ALU.mult, op1=ALU.mult), (vg_b, rs8_b, GVB[1]), (vn_b,))
                      su_tm(w, w_b, ev)
                      su_done(s)
                  ut, ut_b = UT
                  for s in range(3, 6):
                      w, w_b = WIN[s % 2]

                      def ev(sub, ps, pb, g=s - 3):
                          self.act(ut[:, g * 4 + sub, :], ps, AF.Gelu, pb, (ut_b,))
                      su_fm(w, w_b, ev)
                      su_done(s)
                  qmt, qmt_b = QMT
                  w, w_b = WIN[6 % 2]

                  def ev(sub, ps, pb):
                      self.act(qmt[:, sub, :], ps, AF.Copy, pb, (qmt_b,))
                  su_fm(w, w_b, ev)
                  su_done(6)
                  mem_prepare(l, wd, 7, WIN, MEMN, KMT, VM, XCm)
                  cat, cat_b = CAT
                  mem_attention(QMT, KMT, VM, CAT, 12, PT, RC)
                  rec.dma("pool", WST[0], wst_d[:, :].rearrange("q (g p) -> q g p", p=128), (), (WST[1],), WST[1])
                  wst, wst_b = WST
                  sbb, sbb_b = SBB
                  it = 0
                  for gg in range(12):
                      for cq in range(2):
                          bk = 4 + (it % 3)
                          tb, tb_b = TMPB[it % 2]
                          it += 1
                          for cc in range(4):
                              self.mm(self.PS(bk, 128, cc * 128), vn[:, cq * 4 + cc, gg * 128:(gg + 1) * 128], wst[:, gg, :],
                                      True, True, (vn_b, wst_b), (self.bank[bk],))
                          self.dve(lambda e, tb=tb, bk=bk, gg=gg, sbb=sbb: e.tensor_tensor(
                              out=tb.rearrange("p (c q) -> p c q", q=128),
                              in0=self.PS(bk).rearrange("p (c q) -> p c q", q=128),
                              in1=sbb[:, gg:gg + 1, :].to_broadcast([128, 4, 128]), op=ALU.add),
                              (self.bank[bk], sbb_b), (tb_b,))
                          self.dve(lambda e, tb=tb, gg=gg, cq=cq, cat=cat, ut=ut: e.tensor_tensor(
                              out=cat[:, gg, cq * 512:(cq + 1) * 512], in0=tb, in1=ut[:, gg, cq * 512:(cq + 1) * 512],
                              op=ALU.mult), (tb_b, ut_b), (cat_b,))
                  nk_out = 16
              cat, cat_b = CAT
              WO = [self.SB("woA%d" % i, B0 + 98368 + 4096 * i, [128, nk_out, 128], BF16) for i in range(4)] if isA else \
                   [self.SB("woB%d" % i, B0 + 114752 + 4096 * i, [128, nk_out, 128], BF16) for i in range(4)]
              XCo = [self.SB("xc%d" % i, ARENA - 8192 - 8192 + 4096 * i, [128, T], F32) for i in range(2)]
              xoff = (B0 + 147520) if isA else (B0 + 131136)
              XCo += [self.SB("xcx%d" % i, xoff + 4096 * i, [128, T], F32) for i in range(2)]
              nsrc = proj_residual(nk_out, (lambda k, hf, cat=cat: cat[:, k, hf * 512:(hf + 1) * 512]), cat_b, wout_d[l],
                                   WO, XCo, cur, cur_b, xd, self.xd_b, [(4, 5), (6, 7), (2, 3)],
                                   stats=True, prefetch=True)
              cur, cur_b = xd, self.xd_b
              chk(7)

              HB = self.SB("hb", ARENA - 8192 - 8192 - 256, [128, 2, 16], BF16)
              HBL = self.SB("hbl", ARENA - 8192 - 8192 - 192, [128, 16], BF16)
              HBR = self.SB("hbr", ARENA - 8192 - 8192 - 128, [128, 16], BF16)
              hb, hb_b = HB
              XSv = self.V(XSo, [128, KD, T], F32)
              xs_all = tuple(b_ for (_, b_) in XS)
              gl = gains[:, (4 + l) * 16:(4 + l) * 16 + 16]

              def early_halo(sources, hb=hb, hb_b=hb_b, HBL=HBL, HBR=HBR, gl=gl, l=l):
                  for ti, col in ((0, 0), (1, T - 1)):
                      self.dve(lambda e, ti=ti, col=col: e.scalar_tensor_tensor(
                          out=hb[:, ti, 0:12], in0=XSv[:, 0:12, col], scalar=RSTD[:, col:col + 1],
                          in1=gl[:, 0:12], op0=ALU.mult, op1=ALU.mult), xs_all + (RSTD_b, gains_b), (hb_b,))
                      for k in range(12, 16):
                          sa, sb_ = sources[k]
                          self.dve(lambda e, ti=ti, col=col, k=k, sa=sa: e.scalar_tensor_tensor(
                              out=hb[:, ti, k:k + 1], in0=sa[:, col:col + 1], scalar=RSTD[:, col:col + 1],
                              in1=gl[:, k:k + 1], op0=ALU.mult, op1=ALU.mult), (sb_, RSTD_b, gains_b), (hb_b,))
                  rec.dma("sp", hbloc[:, :], self.V(ARENA - 8192 - 8192 - 256, [128, 32], BF16), (hb_b,), (hblocb,), hb_b)
                  rec.collective(hbloc.ap().opt(), hball.ap().opt(), (hblocb,), (hballb,), cch, GROUPS)
                  rec.dma("sp", HBL[0], hball[0:128, 16:32], (hballb,), (HBL[1],), HBL[1])
                  rec.dma("sp", HBR[0], hball[128:256, 0:16], (hballb,), (HBR[1],), HBR[1])
              norm_to_HT(cur, cur_b, 4 + l, early=early_halo, preloaded=True, sources=nsrc)
              self.dve(lambda e, HBL=HBL: e.tensor_scalar(HT[:, :, 0], HBL[0], hm[:, 0:1], None, op0=ALU.mult),
                       (HBL[1], hm_b), (HT_b,))
              self.dve(lambda e, HBR=HBR: e.tensor_scalar(HT[:, :, TE - 1], HBR[0], hm[:, 1:2], None, op0=ALU.mult),
                       (HBR[1], hm_b), (HT_b,))
              G, G_b = self.SB("G", B0 + 32832, [128, 44, T], BF16)
              GB = [self.SB("Gk%d" % jj, B0 + 32832 + 2048 * jj, [128, T], BF16)[1] for jj in range(44)]
              WUP = [self.SB("wup%d" % i, B0 + 122944 + 4096 * i, [128, KD, 128], BF16) for i in range(5)]
              ASB = [self.SB("asb%d" % i, B0 + 143424 + 4128 * i, [128, TE], F32) for i in range(2)]
              CB = [self.SB("cb%d" % i, B0 + 151680 + 4096 * i, [128, T], F32) for i in range(3)]
              WDN0 = self.SB("wdn0", B0 + 163968, [128, 44, 128], BF16)
              WDN1 = self.SB("wdn1", B0 + 122944, [128, 44, 128], BF16)
              XCf = [self.SB("xc%d" % i, ARENA - 8192 - 8192 + 4096 * i, [128, T], F32) for i in range(2)]
              XCf += [self.SB("xcf%d" % i, B0 + 143424 + 4096 * i, [128, T], F32) for i in range(2)]
              wu = wup_d[l]
              chk(8)
              for u in range(5):
                  rec.dma("pool", WUP[u][0], wu[u, :, :, :], (), (WUP[u][1],), WUP[u][1])
              gate_c = None
              for u in range(NU):
                  w, w_b = WUP[u % 5]
                  bs = (0, 1, 2) if u % 2 == 0 else (3, 4, 5)
                  for k in range(KD):
                      for tg in range(3):
                          self.mm(self.PS(bs[tg], 342), w[:, k, :], HT[:, k, tg * 342:(tg + 1) * 342], k == 0, k == KD - 1,
                                  (w_b, HT_b), (self.bank[bs[tg]],))
                  if u + 5 < NU:
                      rec.dma("pool", w, wu[u + 5, :, :, :], (), (w_b,), w_b)
                  if u == 40:
                      rec.dma("pool", WDN0[0], wdn_d[l][0, :, :, :], (), (WDN0[1],), WDN0[1])
                  a, a_b = ASB[u % 2]
                  for tg in range(3):
                      self.act(a[:, tg * 342:(tg + 1) * 342], self.PS(bs[tg], 342), AF.Copy, (self.bank[bs[tg]],), (a_b,))
                  c, c_b = CB[u % 3]
                  cp = [convp[:, l, u, q:q + 1] for q in range(4)]
                  self.dve(lambda e, c=c, a=a, cp=cp: e.tensor_scalar(c, a[:, 1:1 + T], cp[1], cp[3], op0=ALU.mult,
                                                                       op1=ALU.add), (a_b, convp_b), (c_b,))
                  self.dve(lambda e, c=c, a=a, cp=cp: e.scalar_tensor_tensor(out=c, in0=a[:, 0:T], scalar=cp[0], in1=c,
                                                                              op0=ALU.mult, op1=ALU.add),
                           (a_b, c_b, convp_b), (c_b,))
                  self.dve(lambda e, c=c, a=a, cp=cp: e.scalar_tensor_tensor(out=c, in0=a[:, 2:2 + T], scalar=cp[2], in1=c,
                                                                              op0=ALU.mult, op1=ALU.add),
                           (a_b, c_b, convp_b), (c_b,))
                  if u % 2 == 0:
                      self.act(c, c, AF.Gelu, (c_b,), (c_b,))
                      gate_c = (c, c_b)
                  else:
                      gc, gc_b = gate_c
                      self.dve(lambda e, gc=gc, c=c, jj=u // 2, G=G: e.tensor_tensor(out=G[:, jj, :], in0=gc, in1=c, op=ALU.mult),
                               (gc_b, c_b), (GB[u // 2],))
              msrc = proj_residual(44, (lambda k, hf, G=G: G[:, k, hf * 512:(hf + 1) * 512]), (lambda k, GB=GB: GB[k]), wdn_d[l], [WDN0, WDN1],
                                   XCf, cur, cur_b, xd, self.xd_b, [(6, 7), (2, 3), (4, 5)], stats=True, preloaded=1,
                                   xs=XSM)

        except _Stop:
            pass

        yb = [Buf("y%d" % k) for k in range(KD)]
        if self.final_norm:
            XCy = [self.SB("xc%d" % i, ARENA - 8192 - 8192 + 4096 * i, [128, T], F32) for i in range(2)]

            def o(k, xs, xs_b, g):
                xc, xc_b = XCy[k % 2]
                self.dve(lambda e, xs=xs, g=g, xc=xc: e.scalar_tensor_tensor(out=xc, in0=xs, scalar=g, in1=RSTD,
                                                                             op0=ALU.mult, op1=ALU.mult),
                         (xs_b, RSTD_b, gains_b), (xc_b,))
                rec.dma("sp", y_out[k, :, :], xc, (xc_b,), (yb[k],), xc_b)
            norm(cur, cur_b, 12, o, preloaded=(msrc is not None), sources=msrc)
        else:
            XCy = [self.SB("xc%d" % i, ARENA - 8192 - 8192 + 4096 * i, [128, T], F32) for i in range(2)]
            for k in range(KD):
                xc, xc_b = XCy[k % 2]
                rec.dma("sp", xc, cur[k, :, :], (cur_b[k],), (xc_b,), xc_b)
                rec.dma("sp", y_out[k, :, :], xc, (xc_b,), (yb[k],), xc_b)
        rec.finish()
        rec.finalize()
        with nc.Block() as block:
            @block.tensor
            def _(e):
                rec.emit("pe", e)

            @block.scalar
            def _(e):
                rec.emit("act", e)

            @block.vector
            def _(e):
                rec.emit("dve", e)

            @block.gpsimd
            def _(e):
                rec.emit("pool", e)

            @block.sync
            def _(e):
                rec.emit("sp", e)
        self.stack.close()
        return nc


def _su(W, cols):
    Wc = W[:, cols]
    n = Wc.shape[1] // 512
    return np.ascontiguousarray(Wc.reshape(KD, 128, n, 512).transpose(2, 1, 0, 3))


def _masks(c):
    k = np.arange(128)[:, None]
    q = np.arange(128)[None, :]
    out = np.zeros((128, 2, 3, 3, 128), np.float32)
    for gi, d in enumerate((1, 4)):
        for v in range(3):
            for di in range(3):
                dj = 128 * (di - 1) + k - q
                ok = np.abs(dj) <= 64
                if v == 0 and di == 0 and c == 0:
                    ok = np.zeros_like(ok)
                if v == 2 and di == 2 and c == 1:
                    ok = np.zeros_like(ok)
                out[:, gi, v, di, :] = np.where(ok, d * np.abs(dj), BIG)
    dt01 = out.reshape(128, 2 * 3 * 384)
    kk = np.arange(128)[:, None]
    qq = np.arange(64)[None, :]
    dj = kk - (64 * c + qq)
    dt2 = np.where(np.abs(dj) <= 64, 16.0 * np.abs(dj), BIG).astype(np.float32)
    hm = np.zeros((128, 2), np.float32)
    hm[:, 0] = float(c)
    hm[:, 1] = float(1 - c)
    return dt01, dt2, hm


def _prep_shared(inp, layers):
    f = lambda a: np.asarray(a, dtype=np.float32)
    sh = {}
    g = np.concatenate([f(inp["mix_norm_g"]), f(inp["ffn_norm_g"]), f(inp["mem_norm_g"]),
                        f(inp["final_norm_g"])[None, :]], axis=0)
    sh["gains"] = np.ascontiguousarray(g.reshape(13, KD, 128).transpose(2, 0, 1).reshape(128, 13 * KD))
    cw, cb = f(inp["ffn_conv_w"]), f(inp["ffn_conv_b"])
    cp = np.concatenate([cw, cb[:, None, :]], axis=1)
    cp = cp.reshape(4, 4, 2, 44, 128)
    sh["convp"] = np.ascontiguousarray(cp.transpose(4, 0, 3, 2, 1).reshape(128, 4 * NU * 4))
    for l in layers:
        j = l // 2
        wkv = f(inp["w_mem_kv"][l])
        if l % 2 == 0:
            W = f(inp["a_w_in"][j])
            cols = np.concatenate([np.arange(1536, 3072), np.arange(3072, 4608), np.arange(0, 1536),
                                   np.arange(4608, 5120)])
            Wo = f(inp["a_w_out"][j])
            nk = 8
        else:
            W = f(inp["b_w_in"][j])
            cols = np.concatenate([np.arange(1536, 3072), np.arange(0, 1536), np.arange(3072, 3584)])
            Wo = f(inp["b_w_out"][j])
            nk = 16
            sh["gvb%d" % l] = np.ascontiguousarray(np.broadcast_to(f(inp["b_v_norm_g"][j])[None, :], (128, 1536)))
            sb = f(inp["b_s_bias"][j])
            sh["sbb%d" % l] = np.ascontiguousarray(np.broadcast_to(sb.reshape(1, 1536), (128, 1536)))
            ws = f(inp["b_w_s"][j])
            sh["wst%d" % l] = np.ascontiguousarray(ws.transpose(2, 0, 1).reshape(128, 1536))
        sh["win%d" % l] = np.concatenate([_su(W, cols), _su(wkv, np.arange(1024))], axis=0)
        sh["wout%d" % l] = np.ascontiguousarray(Wo.reshape(nk, 128, 16, 128).transpose(2, 1, 0, 3))
        Wu = f(inp["ffn_w_up"][l])
        sh["wup%d" % l] = np.ascontiguousarray(Wu.reshape(KD, 128, 2, 44, 128).transpose(3, 2, 1, 0, 4).reshape(NU, 128, KD, 128))
        Wd = f(inp["ffn_w_down"][l])
        sh["wdn%d" % l] = np.ascontiguousarray(Wd.reshape(44, 128, 16, 128).transpose(2, 1, 0, 3))
    return sh


def _run(inp, layers, final_norm, x_shards):
    b = Builder(layers, final_norm)
    nc = b.build()
    sh = _prep_shared(inp, layers)
    mem = np.asarray(inp["mem"], dtype=np.float32)
    in_maps = []
    for core in range(8):
        bi, c = core // 2, core % 2
        m = dict(sh)
        m["xT"] = x_shards[core]
        m["memT"] = np.ascontiguousarray(mem[bi].T.reshape(KD, 128, 256))
        m["dt01"], m["dt2"], m["hm"] = _masks(c)
        in_maps.append(m)
    res = run_bass_kernel_spmd(nc, in_maps, core_ids=list(range(8)))
    key = "yT" if final_norm else "xo"
    return [np.asarray(r[key]) for r in res.results]


def kernel(**inputs):
    x = np.asarray(inputs["x"], dtype=np.float32)
    shards = []
    for core in range(8):
        bi, c = core // 2, core % 2
        shards.append(np.ascontiguousarray(x[bi, c * T:(c + 1) * T, :].T.reshape(KD, 128, T)))
    outs = _run(inputs, list(range(N_LAYERS)), True, shards)
    y = np.zeros((4, 2048, D), np.float32)
    for core in range(8):
        bi, c = core // 2, core % 2
        y[bi, c * T:(c + 1) * T, :] = outs[core].reshape(D, T).T
    return y
```

```python
import os
import numpy as np
import concourse.bass as bass
import concourse.mybir as mybir
from concourse.bass_utils import run_bass_kernel_spmd

F32 = mybir.dt.float32
BF16 = mybir.dt.bfloat16
U8 = mybir.dt.uint8
AF = mybir.ActivationFunctionType
ALU = mybir.AluOpType

D = 2048
T = 1024
TE = T + 2
KD = 16
FF = 5632
NU = 88
EPS = 1e-6
SCALE = 128 ** -0.5
BIG = 1.0e6
N_LAYERS = 4
ARENA = 212000
PERS = 16384


class Buf:
    __slots__ = ("name", "lw", "rd", "rdd", "dsem", "dcount", "lo", "hi", "ov")

    def __init__(self, name, lo=None, hi=None):
        self.name = name
        self.lw = []
        self.rd = {}
        self.rdd = []
        self.dsem = None
        self.dcount = 0
        self.lo, self.hi = lo, hi
        self.ov = [self]


class Op:
    __slots__ = ("eng", "fn", "is_dma", "deps", "needs_inc", "inc_idx", "pos", "dsem", "dval", "dinc")

    def __init__(self, eng, fn, is_dma):
        self.eng, self.fn, self.is_dma = eng, fn, is_dma
        self.deps = []
        self.needs_inc = False
        self.inc_idx = 0
        self.pos = 0
        self.dsem = None
        self.dval = 0
        self.dinc = 16


class Rec:
    ENGS = ("pe", "act", "dve", "pool", "sp")

    def __init__(self, nc, stack):
        self.nc = nc
        self.stack = stack
        self.q = {e: [] for e in self.ENGS}
        self.esem = {e: stack.enter_context(nc.semaphore("es_" + e)) for e in self.ENGS}
        self.sbufs = []
        self.nsem = 0
        self.all_dma_bufs = []

    def sbuf(self, name, lo, hi):
        b = Buf(name, lo, hi)
        for o in self.sbufs:
            if o.lo < hi and lo < o.hi:
                o.ov.append(b)
                b.ov.append(o)
        self.sbufs.append(b)
        return b

    def _dsem(self, b):
        if b.dsem is None:
            b.dsem = self.stack.enter_context(self.nc.semaphore("ds%d" % self.nsem))
            self.nsem += 1
            self.all_dma_bufs.append(b)
        return b.dsem

    def _deps(self, op, reads, writes):
        cand = []
        for b in reads:
            for o in b.ov:
                cand += o.lw
        for b in writes:
            for o in b.ov:
                cand += o.lw
                cand += list(o.rd.values())
                cand += o.rdd
        seen = set()
        for p in cand:
            if p is op or id(p) in seen:
                continue
            seen.add(id(p))
            if (not p.is_dma) and p.eng == op.eng and not op.is_dma:
                if op.eng == "pe" or op.pos - p.pos > 3:
                    continue
            op.deps.append(p)
            if not p.is_dma:
                p.needs_inc = True
        for b in reads:
            if op.is_dma:
                b.rdd.append(op)
                if len(b.rdd) > 8:
                    b.rdd = b.rdd[-8:]
            else:
                b.rd[op.eng] = op
        for b in writes:
            b.lw = [op]
            b.rd = {}
            b.rdd = []

    def compute(self, eng, fn, reads=(), writes=()):
        op = Op(eng, fn, False)
        op.pos = len(self.q[eng])
        self._deps(op, reads, writes)
        self.q[eng].append(op)
        return op

    def dma(self, q, out_ap, in_ap, reads, writes, sem_buf):
        op = Op(q, (lambda e, o=out_ap, i=in_ap: e.dma_start(out=o, in_=i)), True)
        op.pos = len(self.q[q])
        op.dsem = self._dsem(sem_buf)
        sem_buf.dcount += 16
        op.dval = sem_buf.dcount
        self._deps(op, reads, writes)
        self.q[q].append(op)
        return op

    def collective(self, in_ap, out_ap, reads, writes, sem_buf, groups):
        def fn(e, i=in_ap, o=out_ap):
            return e.collective_compute("AllGather", ALU.bypass, replica_groups=groups, ins=[i], outs=[o])
        op = Op("pool", fn, True)
        op.dinc = 1
        op.pos = len(self.q["pool"])
        op.dsem = self._dsem(sem_buf)
        sem_buf.dcount += 1
        op.dval = sem_buf.dcount
        self._deps(op, reads, writes)
        self.q["pool"].append(op)
        return op

    def finish(self):
        op = Op("sp", None, True)
        for b in self.all_dma_bufs:
            fake = Op("sp", None, True)
            fake.dsem, fake.dval = b.dsem, b.dcount
            op.deps.append(fake)
        self.q["sp"].append(op)

    def finalize(self):
        for e in self.ENGS:
            c = 0
            for op in self.q[e]:
                if (not op.is_dma) and op.needs_inc:
                    c += 1
                    op.inc_idx = c

    def emit(self, eng, e):
        obs = {}
        for op in self.q[eng]:
            for p in op.deps:
                if p.is_dma:
                    key, val = p.dsem, p.dval
                else:
                    key, val = self.esem[p.eng], p.inc_idx
                if obs.get(key, 0) < val:
                    e.wait_ge(key, val)
                    obs[key] = val
            if op.fn is None:
                continue
            ins = op.fn(e)
            if op.is_dma:
                ins.then_inc(op.dsem, op.dinc)
            elif op.needs_inc:
                ins.then_inc(self.esem[eng], 1)


class _Stop(Exception):
    pass


class Builder:
    def __init__(self, layers, final_norm, n_a_su=12, n_b_su=9):
        import contextlib
        self.layers = layers
        self.stack = contextlib.ExitStack()
        nc = self.nc = bass.Bass("TRN2", target_bir_lowering=False)
        self.rec = Rec(nc, self.stack)
        self.arena = nc.alloc_sbuf_tensor("arena", [128, ARENA], U8)
        self.psum = nc.alloc_psum_tensor("ps", [128, 4096], F32)
        self.bank = [Buf("bank%d" % i) for i in range(8)]
        self.dram = {}
        self._sbc = {}
        self.final_norm = final_norm

    def V(self, off, shape, dt):
        sz = 4 if dt == F32 else 2
        n = int(np.prod(shape[1:]))
        ap = self.arena[:, off:off + n * sz].bitcast(dt)
        if len(shape) == 3:
            ap = ap.rearrange("p (a b) -> p a b", b=shape[2])
        elif len(shape) == 4:
            ap = ap.rearrange("p (a b c) -> p a b c", b=shape[2], c=shape[3])
        return ap

    def SB(self, name, off, shape, dt):
        sz = 4 if dt == F32 else 2
        n = int(np.prod(shape[1:]))
        key = (name, off, n * sz)
        if key not in self._sbc:
            assert off + n * sz <= ARENA, (name, off, n * sz)
            self._sbc[key] = self.rec.sbuf(name, off, off + n * sz)
        return self.V(off, shape, dt), self._sbc[key]

    def PS(self, b, n=512, off=0):
        return self.psum[:, b * 512 + off: b * 512 + off + n]

    def din(self, name, shape, dt=F32):
        t = self.nc.dram_tensor(name, list(shape), dt, kind="ExternalInput")
        self.dram[name] = t
        return t

    def mm(self, out, lhsT, rhs, start, stop, reads, writes, skip=False):
        self.rec.compute("pe", (lambda e, o=out, l=lhsT, r=rhs, s=start, st=stop, sk=skip:
                                e.matmul(o, l, r, start=s, stop=st, skip_group_check=sk)), reads, writes)

    def act(self, out, in_, func, reads, writes, scale=1.0, bias=0.0, accum=None):
        if accum is None:
            fn = (lambda e, o=out, i=in_, f=func, s=scale, b=bias: e.activation(o, i, f, bias=b, scale=s))
        else:
            fn = (lambda e, o=out, i=in_, f=func, s=scale, b=bias, a=accum:
                  e.activation(o, i, f, bias=b, scale=s, accum_out=a))
        self.rec.compute("act", fn, reads, writes)

    def dve(self, fn, reads, writes):
        self.rec.compute("dve", fn, reads, writes)

    def build(self):
        nc, rec = self.nc, self.rec
        L = self.layers
        x_in = self.din("xT", [KD, 128, T])
        memT = self.din("memT", [KD, 128, 256])
        gains_d = self.din("gains", [128, 13 * 16])
        convp_d = self.din("convp", [128, 4 * NU * 4])
        dt01_d = self.din("dt01", [128, 2 * 3 * 384])
        dt2_d = self.din("dt2", [128, 64])
        hm_d = self.din("hm", [128, 2])
        win_d, wout_d, wup_d, wdn_d, bc_d = {}, {}, {}, {}, {}
        for l in L:
            isA = (l % 2 == 0)
            win_d[l] = self.din("win%d" % l, [12 if isA else 9, 128, KD, 512])
            wout_d[l] = self.din("wout%d" % l, [16, 128, 8 if isA else 16, 128])
            wup_d[l] = self.din("wup%d" % l, [NU, 128, KD, 128])
            wdn_d[l] = self.din("wdn%d" % l, [16, 128, 44, 128])
            if not isA:
                bc_d[l] = (self.din("gvb%d" % l, [128, 1536]), self.din("sbb%d" % l, [128, 1536]),
                           self.din("wst%d" % l, [128, 1536]))
        if self.final_norm:
            y_out = nc.dram_tensor("yT", [KD, 128, T], F32, kind="ExternalOutput")
        else:
            y_out = nc.dram_tensor("xo", [KD, 128, T], F32, kind="ExternalOutput")
        xd = nc.dram_tensor("xd", [KD, 128, T], F32)
        kloc = [nc.dram_tensor("kloc%d" % g, [512, T], BF16) for g in range(3)]
        vloc = [nc.dram_tensor("vloc%d" % g, [T, 512], BF16) for g in range(3)]
        kall = [nc.dram_tensor("kall%d" % g, [1024, T], BF16) for g in range(3)]
        vall = [nc.dram_tensor("vall%d" % g, [2 * T, 512], BF16) for g in range(3)]
        hbloc = nc.dram_tensor("hbloc", [128, 32], BF16)
        hball = nc.dram_tensor("hball", [256, 32], BF16)
        self.xd_b = [Buf("xd%d" % k) for k in range(KD)]
        self.xin_b = [Buf("xin%d" % k) for k in range(KD)]
        klocb = [Buf("kloc%d" % i) for i in range(3)]
        vlocb = [Buf("vloc%d" % i) for i in range(3)]
        kallb = [Buf("kall%d" % g) for g in range(3)]
        vallb = [Buf("vall%d" % g) for g in range(3)]
        hblocb, hballb = Buf("hbloc"), Buf("hball")
        cck, ccv, cch = [Buf("cck%d" % g) for g in range(3)], [Buf("ccv%d" % g) for g in range(3)], Buf("cch")
        GROUPS = [[0, 1], [2, 3], [4, 5], [6, 7]]

        ones, ones_b = self.SB("ones", 0, [128, 128], BF16)
        gains, gains_b = self.SB("gains", 256, [128, 208], F32)
        convp, convp_b = self.SB("convp", 1088, [128, 4, NU, 4], F32)
        dt01, dt01_b = self.SB("dt01", 6720, [128, 6, 384], F32)
        dt2, dt2_b = self.SB("dt2", 15936, [128, 64], F32)
        hm, hm_b = self.SB("hm", 16192, [128, 2], F32)
        rec.compute("dve", lambda e: e.memset(ones, 1.0), (), (ones_b,))
        EPSB, epsb_b = self.SB("epsb", 16224, [128, 1], F32)
        rec.compute("dve", lambda e: e.memset(EPSB, EPS), (), (epsb_b,))
        rec.dma("sp", gains, gains_d[:, :], (), (gains_b,), gains_b)
        rec.dma("sp", self.V(1088, [128, 4 * NU * 4], F32), convp_d[:, :], (), (convp_b,), convp_b)
        rec.dma("sp", self.V(6720, [128, 2304], F32), dt01_d[:, :], (), (dt01_b,), dt01_b)
        rec.dma("sp", dt2, dt2_d[:, :], (), (dt2_b,), dt2_b)
        rec.dma("sp", hm, hm_d[:, :], (), (hm_b,), hm_b)
        self.ones, self.ones_b = ones, ones_b
        self.gains, self.gains_b = gains, gains_b

        B0 = PERS
        HT, HT_b = self.SB("HT", B0, [128, KD, TE], BF16)
        HTK = [self.rec.sbuf("HTk%d" % k, B0 + 2052 * k, B0 + 2052 * (k + 1)) for k in range(KD)]
        POOL_K = ()
        NRM = ARENA - 8192
        SQ = [self.SB("sq%d" % i, NRM + 2048 * i, [128, T], BF16) for i in range(2)]
        RSTD, RSTD_b = self.SB("rstd", NRM + 4096, [128, T], F32)
        XSo = B0 + 32832
        XS = [self.SB("xs%d" % k, XSo + 4096 * k, [128, T], F32) for k in range(KD)]

        XSM = [self.SB("xsm%d" % k, B0 + 81984 + 4096 * k, [128, T], F32) for k in range(8)] + \
              [XS[k] for k in range(8, KD)]

        def norm_load(src, src_b, k, xs=None):
            xs = XS if xs is None else xs
            rec.dma("sp", xs[k][0], src[k, :, :], (src_b[k],), (xs[k][1],), xs[k][1])

        def norm_stats(k):
            sq, sq_b = SQ[k % 2]
            self.act(sq, XS[k][0], AF.Square, (XS[k][1],), (sq_b,))
            for hf in range(2):
                self.mm(self.PS(hf), ones, sq[:, hf * 512:(hf + 1) * 512], k == 0, k == KD - 1,
                        (ones_b, sq_b), (self.bank[hf],))

        def norm_stats_from(k, x_ap, x_b):
            sq, sq_b = SQ[k % 2]
            self.act(sq, x_ap, AF.Square, (x_b,), (sq_b,))

            def pe_part(k=k, sq=sq, sq_b=sq_b):
                for hf in range(2):
                    self.mm(self.PS(hf), ones, sq[:, hf * 512:(hf + 1) * 512], k == 0, k == KD - 1,
                            (ones_b, sq_b), (self.bank[hf],))
            return pe_part

        def norm(src, src_b, gcol, out_fn, early=None, preloaded=False, sources=None):
            if sources is None:
                sources = [XS[k] for k in range(KD)]
            if not preloaded:
                for k in range(KD):
                    norm_load(src, src_b, k)
                for k in range(KD):
                    norm_stats(k)
            tmp = self.psum[:, 0:1024]
            self.act(RSTD, tmp, AF.Ln, (self.bank[0], self.bank[1]), (RSTD_b,), scale=1.0 / D, bias=EPSB)
            self.act(RSTD, RSTD, AF.Exp, (RSTD_b,), (RSTD_b,), scale=-0.5)
            if early is not None:
                early(sources)
            for k in [12, 13, 14, 15] + list(range(12)):
                out_fn(k, sources[k][0], sources[k][1], gains[:, gcol * 16 + k: gcol * 16 + k + 1])

        def norm_to_HT(src, src_b, gcol, early=None, preloaded=False, sources=None):
            def o(k, xs, xs_b, g):
                eng = "pool" if k in POOL_K else "dve"
                rec.compute(eng, lambda e, k=k, xs=xs, g=g: e.scalar_tensor_tensor(
                    out=HT[:, k, 1:1 + T], in0=xs, scalar=g, in1=RSTD, op0=ALU.mult, op1=ALU.mult),
                    (xs_b, RSTD_b, gains_b), (HTK[k],))
            norm(src, src_b, gcol, o, early, preloaded, sources)

        def proj_residual(nk, rhs, rhs_b, w_d, wslots, xc, src, src_b, dst, dst_b, bank_sets,
                          stats=False, prefetch=False, preloaded=0, xs=None):
            nw = len(wslots)
            nx = len(xc)
            assert nx == 4
            for m in range(preloaded, min(nw, 16)):
                rec.dma("pool", wslots[m][0], w_d[m, :, :, :], (), (wslots[m][1],), wslots[m][1])

            def loadx(m):
                x, x_b = xc[m % nx]
                rec.dma("sp", x, src[m, :, :], (src_b[m],), (x_b,), x_b)
            for m in range(nx - 2):
                loadx(m)
            pend = []
            for m in range(16):
                if m + nx - 2 < 16:
                    loadx(m + nx - 2)
                if prefetch and m >= 2 and m - 2 < 12:
                    norm_load(dst, dst_b, m - 2)
                while pend and pend[0][0] <= m - 2:
                    pend.pop(0)[1]()
                w, w_b = wslots[m % nw]
                bs = bank_sets[m % len(bank_sets)]
                x, x_b = xc[m % nx]
                for k in range(nk):
                    for hf in range(2):
                        self.mm(self.PS(bs[hf]), w[:, k, :], rhs(k, hf), k == 0, k == nk - 1,
                                (w_b, rhs_b(k) if callable(rhs_b) else rhs_b), (self.bank[bs[hf]],))
                if m + nw < 16:
                    rec.dma("pool", w, w_d[m + nw, :, :, :], (), (w_b,), w_b)
                for hf in range(2):
                    self.dve(lambda e, x=x, hf=hf, b=bs[hf]: e.tensor_tensor(
                        out=x[:, hf * 512:(hf + 1) * 512], in0=self.PS(b), in1=x[:, hf * 512:(hf + 1) * 512],
                        op=ALU.add), (self.bank[bs[hf]], x_b), (x_b,))
                if stats:
                    pend.append((m, norm_stats_from(m, x, x_b)))
                rec.dma("sp", dst[m, :, :], x, (x_b,), (dst_b[m],), x_b)
            while pend:
                pend.pop(0)[1]()
            xs = XS if xs is None else xs
            if stats and not prefetch:
                for k in range(12):
                    norm_load(dst, dst_b, k, xs)
            return [xs[k] for k in range(12)] + [xc[k % nx] for k in range(12, 16)]

        def mem_prepare(l, w_d, su0, WIN, MEMN, KMT, VM, XCm):
            memn, memn_b = MEMN
            ms, ms_b = self.SB("memstage", B0, [128, KD, 256], F32)
            rec.dma("sp", ms, memT.ap().rearrange("k p t -> p k t"), (), (ms_b,), ms_b)
            for k in range(KD):
                sq, sq_b = SQ[k % 2]
                self.act(sq[:, 0:256], ms[:, k, :], AF.Square, (ms_b,), (sq_b,))
                self.mm(self.PS(6, 256), ones, sq[:, 0:256], k == 0, k == KD - 1, (ones_b, sq_b), (self.bank[6],))
            rs = RSTD[:, 0:256]
            self.act(rs, self.PS(6, 256), AF.Ln, (self.bank[6],), (RSTD_b,), scale=1.0 / D, bias=EPSB)
            self.act(rs, rs, AF.Exp, (RSTD_b,), (RSTD_b,), scale=-0.5)
            for k in range(KD):
                g = gains[:, (8 + l) * 16 + k: (8 + l) * 16 + k + 1]
                self.dve(lambda e, k=k, g=g: e.scalar_tensor_tensor(
                    out=memn[:, k, :], in0=ms[:, k, :], scalar=g, in1=rs, op0=ALU.mult, op1=ALU.mult),
                    (ms_b, RSTD_b, gains_b), (memn_b,))
            kmt, kmt_b = KMT
            vm, vm_b = VM
            w, w_b = WIN[su0 % 2]
            for hh in range(4):
                for k in range(KD):
                    self.mm(self.PS(6, 256), w[:, k, hh * 128:(hh + 1) * 128], memn[:, k, :], k == 0, k == KD - 1,
                            (w_b, memn_b), (self.bank[6],))
                self.act(kmt[:, hh, :], self.PS(6, 256), AF.Copy, (self.bank[6],), (kmt_b,))
            w, w_b = WIN[(su0 + 1) % 2]
            for tt in range(2):
                for k in range(KD):
                    self.mm(self.PS(7), memn[:, k, tt * 128:(tt + 1) * 128], w[:, k, :], k == 0, k == KD - 1,
                            (w_b, memn_b), (self.bank[7],))
                self.act(vm[:, tt, :], self.PS(7), AF.Copy, (self.bank[7],), (vm_b,))

        def mem_attention(QMT, KMT, VM, CAT, c0, PT, RC):
            qmt, qmt_b = QMT
            kmt, kmt_b = KMT
            vm, vm_b = VM
            cat, cat_b = CAT
            rc, rc_b = RC
            for hh in range(4):
                items = [(hf, kt) for hf in range(2) for kt in range(2)]
                sb_of = lambda i: 4 + (i % 3)

                def s_mm(i):
                    hf, kt = items[i]
                    self.mm(self.PS(sb_of(i)), kmt[:, hh, kt * 128:(kt + 1) * 128],
                            qmt[:, hh, hf * 512:(hf + 1) * 512], True, True, (kmt_b, qmt_b), (self.bank[sb_of(i)],))
                s_mm(0)
                for i in range(4):
                    if i + 1 < 4:
                        s_mm(i + 1)
                    hf, kt = items[i]
                    pt, pt_b = PT[i % len(PT)]
                    self.act(pt, self.PS(sb_of(i)), AF.Exp, (self.bank[sb_of(i)],), (pt_b,), scale=SCALE)
                    self.mm(self.PS(hf), vm[:, kt, hh * 128:(hh + 1) * 128], pt, kt == 0, kt == 1,
                            (vm_b, pt_b), (self.bank[hf],))
                    self.mm(self.PS(2 + hf), ones, pt, kt == 0, kt == 1, (ones_b, pt_b), (self.bank[2 + hf],))
                self.act(rc, self.psum[:, 1024:2048], AF.Ln, (self.bank[2], self.bank[3]), (rc_b,))
                self.act(rc, rc, AF.Exp, (rc_b,), (rc_b,), scale=-1.0)
                self.dve(lambda e, hh=hh: e.tensor_tensor(out=cat[:, c0 + hh, :], in0=self.psum[:, 0:1024], in1=rc,
                                                          op=ALU.mult), (self.bank[0], self.bank[1], rc_b), (cat_b,))

        def su_fm(w, w_b, evac):
            for sub in range(4):
                bs = (4, 5) if sub % 2 == 0 else (6, 7)
                for k in range(KD):
                    for hf in range(2):
                        self.mm(self.PS(bs[hf]), w[:, k, sub * 128:(sub + 1) * 128],
                                HT[:, k, 1 + hf * 512: 1 + (hf + 1) * 512], k == 0, k == KD - 1,
                                (w_b, HT_b), (self.bank[bs[hf]],))
                evac(sub, self.psum[:, bs[0] * 512: bs[0] * 512 + 1024], (self.bank[bs[0]], self.bank[bs[1]]))

        def su_tm(w, w_b, evac):
            for tt in range(8):
                b = tt % 4
                for k in range(KD):
                    self.mm(self.PS(b), HT[:, k, 1 + tt * 128: 1 + (tt + 1) * 128], w[:, k, :], k == 0, k == KD - 1,
                            (w_b, HT_b), (self.bank[b],))
                evac(tt, self.PS(b), (self.bank[b],))

        cur, cur_b = x_in, self.xin_b
        msrc = None
        stop_at = int(os.environ.get("KSTOP", "99"))

        def chk(n):
            if n >= stop_at:
                raise _Stop()
        try:
          for l in L:
              isA = (l % 2 == 0)
              j = l // 2
              nsu = 12 if isA else 9
              wd = win_d[l]
              if msrc is None:
                  norm_to_HT(cur, cur_b, l)
              else:
                  norm_to_HT(cur, cur_b, l, preloaded=True, sources=msrc)
              chk(1)
              WIN = [self.SB("win%d" % i, B0 + 32832 + 16384 * i, [128, KD, 512], BF16) for i in range(2)]
              su_next = [0]

              def su_load(s):
                  w, w_b = WIN[s % 2]
                  rec.dma("pool", w, wd[s, :, :, :], (), (w_b,), w_b)

              su_load(0)
              su_load(1)

              def su_done(s):
                  if s + 2 < nsu:
                      su_load(s + 2)

              XCm = [self.SB("xc%d" % i, ARENA - 8192 - 8192 + 4096 * i, [128, T], F32) for i in range(2)]
              if isA:
                  STG = [self.SB("stg%d" % i, B0 + 65600 + 8192 * i, [128, 4096], BF16) for i in range(4)]
                  QT = self.SB("QT", B0 + 98368, [128, 12, T], BF16)
                  QMT = self.SB("QMT", B0 + 122944, [128, 4, T], BF16)
                  CAT = self.SB("CAT", B0 + 131136, [128, 8, T], BF16)
                  MEMN = self.SB("MEMN", B0 + 163904, [128, KD, 256], BF16)
                  KMT = self.SB("KMT", B0 + 172096, [128, 4, 256], BF16)
                  VM = self.SB("VM", B0 + 174144, [128, 2, 512], BF16)
                  WK = B0 + 86016
                  SBW = [self.SB("sbw%d" % i, WK + 1536 * i, [128, 384], F32) for i in range(3)]
                  PT = [self.SB("pt%d" % i, WK + 4608 + 1024 * i, [128, 512], BF16) for i in range(3)]
                  RC = self.SB("rc", WK + 7680, [128, T], F32)
                  for s in range(3):
                      w, w_b = WIN[s % 2]
                      st, st_b = STG[s % 2]
                      st3 = st.rearrange("p (h t) -> p h t", t=T)

                      def ev(sub, ps, pb, st3=st3, st_b=st_b):
                          self.act(st3[:, sub, :], ps, AF.Copy, pb, (st_b,))
                      su_fm(w, w_b, ev)
                      su_done(s)
                      rec.dma("sp", kloc[s][:, :].rearrange("(h e) t -> e h t", e=128), st3,
                              (st_b,), (klocb[s],), st_b)
                      rec.collective(kloc[s].ap().opt(), kall[s].ap().opt(), (klocb[s],), (kallb[s],), cck[s], GROUPS)
                  chk(2)
                  for s in range(3, 6):
                      w, w_b = WIN[s % 2]
                      st, st_b = STG[2 + s % 2]
                      st3 = st.rearrange("p (t c) -> p t c", c=512)

                      def ev(tt, ps, pb, st3=st3, st_b=st_b):
                          self.act(st3[:, tt, :], ps, AF.Copy, pb, (st_b,))
                      su_tm(w, w_b, ev)
                      su_done(s)
                      g = s - 3
                      rec.dma("sp", vloc[g][:, :].rearrange("(t p) c -> p t c", p=128), st3,
                              (st_b,), (vlocb[g],), st_b)
                      rec.collective(vloc[g].ap().opt(), vall[g].ap().opt(), (vlocb[g],), (vallb[g],), ccv[g], GROUPS)
                  chk(3)
                  qt, qt_b = QT
                  for s in range(6, 9):
                      w, w_b = WIN[s % 2]

                      def ev(sub, ps, pb, g=s - 6):
                          self.act(qt[:, g * 4 + sub, :], ps, AF.Copy, pb, (qt_b,))
                      su_fm(w, w_b, ev)
                      su_done(s)
                  qmt, qmt_b = QMT
                  w, w_b = WIN[9 % 2]

                  def ev(sub, ps, pb):
                      self.act(qmt[:, sub, :], ps, AF.Copy, pb, (qmt_b,))
                  su_fm(w, w_b, ev)
                  su_done(9)
                  mem_prepare(l, wd, 10, WIN, MEMN, KMT, VM, XCm)
                  mem_attention(QMT, KMT, VM, CAT, 4, PT, RC)
                  chk(4)
                  K0 = self.SB("K0", B0 + 0, [128, 4, 1280], BF16)
                  K1 = self.SB("K1", B0 + 10240, [128, 4, 2048], BF16)
                  K2 = self.SB("K2", B0 + 26624, [128, 4, 2048], BF16)
                  V0 = self.SB("V0", B0 + 43008, [128, 10, 512], BF16)
                  V1 = self.SB("V1", B0 + 53248, [128, 4, 4, 512], BF16)
                  V2 = self.SB("V2", B0 + 69632, [128, 16, 512], BF16)
                  kl = lambda g, c0, c1: kloc[g][:, c0:c1].rearrange("(h e) t -> e h t", e=128)
                  ka = lambda R, g, c0, c1: kall[g][R * 512:(R + 1) * 512, c0:c1].rearrange("(h e) t -> e h t", e=128)
                  k0, k0b = K0
                  rec.dma("sp", k0[:, :, 128:1152], kl(0, 0, T), (klocb[0],), (k0b,), k0b)
                  rec.dma("sp", k0[:, :, 0:128], ka(0, 0, 896, 1024), (kallb[0],), (k0b,), k0b)
                  rec.dma("sp", k0[:, :, 1152:1280], ka(1, 0, 0, 128), (kallb[0],), (k0b,), k0b)
                  k1, k1b = K1
                  rec.dma("sp", k1[:, :, 512:1536], kl(1, 0, T), (klocb[1],), (k1b,), k1b)
                  rec.dma("sp", k1[:, :, 0:512], ka(0, 1, 512, 1024), (kallb[1],), (k1b,), k1b)
                  rec.dma("sp", k1[:, :, 1536:2048], ka(1, 1, 0, 512), (kallb[1],), (k1b,), k1b)
                  k2, k2b = K2
                  rec.dma("sp", k2[:, :, 0:1024], ka(0, 2, 0, T), (kallb[2],), (k2b,), k2b)
                  rec.dma("sp", k2[:, :, 1024:2048], ka(1, 2, 0, T), (kallb[2],), (k2b,), k2b)
                  v0, v0b = V0
                  rec.dma("sp", v0[:, 1:9, :], vloc[0][:, :].rearrange("(t p) c -> p t c", p=128), (vlocb[0],), (v0b,), v0b)
                  rec.dma("sp", v0[:, 0, :], vall[0][896:1024, :], (vallb[0],), (v0b,), v0b)
                  rec.dma("sp", v0[:, 9, :], vall[0][1024:1152, :], (vallb[0],), (v0b,), v0b)
                  v1, v1b = V1
                  for a_ in range(2):
                      rec.dma("sp", v1[:, 1 + a_, :, :],
                              vloc[1][a_ * 512:(a_ + 1) * 512, :].rearrange("(p r) c -> p r c", r=4),
                              (vlocb[1],), (v1b,), v1b)
                  rec.dma("sp", v1[:, 0, :, :], vall[1][512:1024, :].rearrange("(p r) c -> p r c", r=4),
                          (vallb[1],), (v1b,), v1b)
                  rec.dma("sp", v1[:, 3, :, :], vall[1][1024:1536, :].rearrange("(p r) c -> p r c", r=4),
                          (vallb[1],), (v1b,), v1b)
                  v2, v2b = V2
                  rec.dma("sp", v2[0:64, :, :], vall[2][0:1024, :].rearrange("(p r) c -> p r c", r=16),
                          (vallb[2],), (v2b,), v2b)
                  rec.dma("sp", v2[64:128, :, :], vall[2][1024:2048, :].rearrange("(p r) c -> p r c", r=16),
                          (vallb[2],), (v2b,), v2b)
                  cat, cat_b = CAT
                  rc, rc_b = RC
                  slopes = [2.0 ** (-8.0 * (i + 1) / 12) for i in range(12)]
                  chk(5)
                  for hh in range(4):
                      blocks = []
                      for b in range(8):
                          v = 0 if b == 0 else (2 if b == 7 else 1)
                          sm = [(k0[:, hh, (b + di) * 128:(b + di + 1) * 128], qt[:, hh, b * 128:(b + 1) * 128], di * 128, 128)
                                for di in range(3)]
                          pv = [(b // 4, (b % 4) * 128, 128, 1, v0[:, b + di, hh * 128:(hh + 1) * 128], di * 128, 128)
                                for di in range(3)]
                          blocks.append((sm, dt01[:, v, :], 384, pv, slopes[hh], (k0b,), (v0b,)))
                      for r in range(4):
                          for b in range(2):
                              v = 3 + (0 if b == 0 else 2)
                              sm = [(k1[:, hh, 512 * (b + di) + r: 512 * (b + di) + r + 509: 4],
                                     qt[:, 4 + hh, 512 * b + r: 512 * b + r + 509: 4], di * 128, 128) for di in range(3)]
                              pv = [(b, r, 128, 4, v1[:, b + di, r, hh * 128:(hh + 1) * 128], di * 128, 128)
                                    for di in range(3)]
                              blocks.append((sm, dt01[:, v, :], 384, pv, slopes[4 + hh], (k1b,), (v1b,)))
                      for r in range(16):
                          sm = [(k2[:, hh, r: r + 2033: 16], qt[:, 8 + hh, r: r + 1009: 16], 0, 64)]
                          pv = [(0, r, 32, 16, v2[:, r, hh * 128:(hh + 1) * 128], 0, 32),
                                (1, r, 32, 16, v2[:, r, hh * 128:(hh + 1) * 128], 32, 32)]
                          blocks.append((sm, dt2[:, :], 64, pv, slopes[8 + hh], (k2b,), (v2b,)))
                      touched = set()

                      def s_stage(i):
                          sm, dist, nc_, pv, slope, kb, vb = blocks[i]
                          bk = 4 + (i % 3)
                          for (lhsT, rhs, co, n) in sm:
                              self.mm(self.PS(bk, n, co), lhsT, rhs, True, True, kb + (qt_b,), (self.bank[bk],))
                      s_stage(0)
                      s_stage(1)
                      for i in range(len(blocks)):
                          if i + 2 < len(blocks):
                              s_stage(i + 2)
                          sm, dist, nc_, pv, slope, kb, vb = blocks[i]
                          bk = 4 + (i % 3)
                          sbw, sbw_b = SBW[i % 3]
                          pt, pt_b = PT[i % 3]
                          self.dve(lambda e, sbw=sbw, dist=dist, nc_=nc_, bk=bk, slope=slope: e.scalar_tensor_tensor(
                              out=sbw[:, 0:nc_], in0=dist, scalar=-slope / SCALE, in1=self.PS(bk, nc_),
                              op0=ALU.mult, op1=ALU.add), (self.bank[bk], dt01_b, dt2_b), (sbw_b,))
                          self.act(pt[:, 0:nc_], sbw[:, 0:nc_], AF.Exp, (sbw_b,), (pt_b,), scale=SCALE)
                          for (ob, c0, n, step, lhsT, pc, pn) in pv:
                              if step == 1:
                                  o_ap = self.PS(ob, n, c0)
                                  d_ap = self.PS(2 + ob, n, c0)
                              else:
                                  o_ap = self.psum[:, ob * 512 + c0: ob * 512 + c0 + (n - 1) * step + 1: step]
                                  d_ap = self.psum[:, (2 + ob) * 512 + c0: (2 + ob) * 512 + c0 + (n - 1) * step + 1: step]
                              first = ob not in touched
                              touched.add(ob)
                              self.mm(o_ap, lhsT, pt[:, pc:pc + pn], first, False, vb + (pt_b,), (self.bank[ob],), skip=True)
                              self.mm(d_ap, ones, pt[:, pc:pc + pn], first, False, (ones_b, pt_b), (self.bank[2 + ob],),
                                      skip=True)
                      self.act(rc, self.psum[:, 1024:2048], AF.Ln, (self.bank[2], self.bank[3]), (rc_b,))
                      self.act(rc, rc, AF.Exp, (rc_b,), (rc_b,), scale=-1.0)
                      self.dve(lambda e, hh=hh, cat=cat, rc=rc: e.tensor_tensor(
                          out=cat[:, hh, :], in0=self.psum[:, 0:1024], in1=rc, op=ALU.mult),
                          (self.bank[0], self.bank[1], rc_b), (cat_b,))
                  nk_out = 8
                  chk(6)
              else:
                  gvb_d, sbb_d, wst_d = bc_d[l]
                  VG = self.SB("VG", B0 + 65600, [128, 8, 1536], F32)
                  UT = self.SB("UT", B0 + 65600, [128, 12, T], F32)
                  VN = self.SB("VN", B0 + 114752, [128, 8, 1536], BF16)
                  QMT = self.SB("QMTb", B0 + 139328, [128, 4, T], BF16)
                  MEMN = self.SB("MEMNb", B0 + 147520, [128, KD, 256], BF16)
                  KMT = self.SB("KMTb", B0 + 155712, [128, 4, 256], BF16)
                  VM = self.SB("VMb", B0 + 157760, [128, 2, 512], BF16)
                  GVB = self.SB("GVB", B0 + 159808, [128, 1536], F32)
                  SBB = self.SB("SBB", B0 + 165952, [128, 12, 128], F32)
                  WST = self.SB("WST", B0 + 172096, [128, 12, 128], BF16)
                  WK = B0 + 175168
                  TMPB = [self.SB("tmpb%d" % i, WK + 2048 * i, [128, 512], F32) for i in range(2)]
                  PT = [self.SB("ptb%d" % i, WK + 4096 + 1024 * i, [128, 512], BF16) for i in range(3)]
                  RC = self.SB("rcb", WK + 7168, [128, T], F32)
                  SS = self.SB("ssb", WK + 11264, [128, 8], F32)
                  RS8 = self.SB("rs8", WK + 11296, [128, 8], F32)
                  CAT = self.SB("CATb", B0 + 0, [128, 16, T], BF16)
                  rec.dma("sp", GVB[0], gvb_d[:, :], (), (GVB[1],), GVB[1])
                  rec.dma("sp", self.V(B0 + 165952, [128, 1536], F32), sbb_d[:, :], (), (SBB[1],), SBB[1])
                  vg, vg_b = VG
                  vn, vn_b = VN
                  ss, ss_b = SS
                  rs8, rs8_b = RS8
                  self.dve(lambda e, ss=ss: e.memset(ss, 0.0), (), (ss_b,))
                  junk, junk_b = self.SB("junk", B0 + 139328, [128, 1536], BF16)
                  for s in range(3):
                      w, w_b = WIN[s % 2]

                      def ev(tt, ps, pb, s=s):
                          self.act(vg[:, tt, s * 512:(s + 1) * 512], ps, AF.Gelu, pb, (vg_b,))
                          if s == 2:
                              self.act(junk, vg[:, tt, :], AF.Square, (vg_b,), (junk_b, ss_b), accum=ss[:, tt:tt + 1])
                              self.act(rs8[:, tt:tt + 1], ss[:, tt:tt + 1], AF.Ln, (ss_b,), (rs8_b,), scale=1.0 / 1536,
                                       bias=EPSB)
                              self.act(rs8[:, tt:tt + 1], rs8[:, tt:tt + 1], AF.Exp, (rs8_b,), (rs8_b,), scale=-0.5)
                              self.dve(lambda e, tt=tt, vn=vn, vg=vg, rs8=rs8, GVB=GVB: e.scalar_tensor_tensor(
                                  out=vn[:, tt, :], in0=vg[:, tt, :], scalar=rs8[:, tt:tt + 1], in1=GVB[0],
                                  op0=ALU.mult, op1=ALU.mult), (vg_b, rs8_b, GVB[1]), (vn_b,))
                      su_tm(w, w_b, ev)
                      su_done(s)
                  ut, ut_b = UT
                  for s in range(3, 6):
                      w, w_b = WIN[s % 2]

                      def ev(sub, ps, pb, g=s - 3):
                          self.act(ut[:, g * 4 + sub, :], ps, AF.Gelu, pb, (ut_b,))
                      su_fm(w, w_b, ev)
                      su_done(s)
                  qmt, qmt_b = QMT
                  w, w_b = WIN[6 % 2]

                  def ev(sub, ps, pb):
                      self.act(qmt[:, sub, :], ps, AF.Copy, pb, (qmt_b,))
                  su_fm(w, w_b, ev)
                  su_done(6)
                  mem_prepare(l, wd, 7, WIN, MEMN, KMT, VM, XCm)
                  cat, cat_b = CAT
                  mem_attention(QMT, KMT, VM, CAT, 12, PT, RC)
                  rec.dma("pool", WST[0], wst_d[:, :].rearrange("q (g p) -> q g p", p=128), (), (WST[1],), WST[1])
                  wst, wst_b = WST
                  sbb, sbb_b = SBB
                  it = 0
                  for gg in range(12):
                      for cq in range(2):
                          bk = 4 + (it % 3)
                          tb, tb_b = TMPB[it % 2]
                          it += 1
                          for cc in range(4):
                              self.mm(self.PS(bk, 128, cc * 128), vn[:, cq * 4 + cc, gg * 128:(gg + 1) * 128], wst[:, gg, :],
                                      True, True, (vn_b, wst_b), (self.bank[bk],))
                          self.dve(lambda e, tb=tb, bk=bk, gg=gg, sbb=sbb: e.tensor_tensor(
                              out=tb.rearrange("p (c q) -> p c q", q=128),
                              in0=self.PS(bk).rearrange("p (c q) -> p c q", q=128),
                              in1=sbb[:, gg:gg + 1, :].to_broadcast([128, 4, 128]), op=ALU.add),
                              (self.bank[bk], sbb_b), (tb_b,))
                          self.dve(lambda e, tb=tb, gg=gg, cq=cq, cat=cat, ut=ut: e.tensor_tensor(
                              out=cat[:, gg, cq * 512:(cq + 1) * 512], in0=tb, in1=ut[:, gg, cq * 512:(cq + 1) * 512],
                              op=ALU.mult), (tb_b, ut_b), (cat_b,))
                  nk_out = 16
              cat, cat_b = CAT
              WO = [self.SB("woA%d" % i, B0 + 98368 + 4096 * i, [128, nk_out, 128], BF16) for i in range(4)] if isA else \
                   [self.SB("woB%d" % i, B0 + 114752 + 4096 * i, [128, nk_out, 128], BF16) for i in range(4)]
              XCo = [self.SB("xc%d" % i, ARENA - 8192 - 8192 + 4096 * i, [128, T], F32) for i in range(2)]
              xoff = (B0 + 147520) if isA else (B0 + 131136)
              XCo += [self.SB("xcx%d" % i, xoff + 4096 * i, [128, T], F32) for i in range(2)]
              nsrc = proj_residual(nk_out, (lambda k, hf, cat=cat: cat[:, k, hf * 512:(hf + 1) * 512]), cat_b, wout_d[l],
                                   WO, XCo, cur, cur_b, xd, self.xd_b, [(4, 5), (6, 7), (2, 3)],
                                   stats=True, prefetch=True)
              cur, cur_b = xd, self.xd_b
              chk(7)

              HB = self.SB("hb", ARENA - 8192 - 8192 - 256, [128, 2, 16], BF16)
              HBL = self.SB("hbl", ARENA - 8192 - 8192 - 192, [128, 16], BF16)
              HBR = self.SB("hbr", ARENA - 8192 - 8192 - 128, [128, 16], BF16)
              hb, hb_b = HB
              XSv = self.V(XSo, [128, KD, T], F32)
              xs_all = tuple(b_ for (_, b_) in XS)
              gl = gains[:, (4 + l) * 16:(4 + l) * 16 + 16]

              def early_halo(sources, hb=hb, hb_b=hb_b, HBL=HBL, HBR=HBR, gl=gl, l=l):
                  for ti, col in ((0, 0), (1, T - 1)):
                      self.dve(lambda e, ti=ti, col=col: e.scalar_tensor_tensor(
                          out=hb[:, ti, 0:12], in0=XSv[:, 0:12, col], scalar=RSTD[:, col:col + 1],
                          in1=gl[:, 0:12], op0=ALU.mult, op1=ALU.mult), xs_all + (RSTD_b, gains_b), (hb_b,))
                      for k in range(12, 16):
                          sa, sb_ = sources[k]
                          self.dve(lambda e, ti=ti, col=col, k=k, sa=sa: e.scalar_tensor_tensor(
                              out=hb[:, ti, k:k + 1], in0=sa[:, col:col + 1], scalar=RSTD[:, col:col + 1],
                              in1=gl[:, k:k + 1], op0=ALU.mult, op1=ALU.mult), (sb_, RSTD_b, gains_b), (hb_b,))
                  rec.dma("sp", hbloc[:, :], self.V(ARENA - 8192 - 8192 - 256, [128, 32], BF16), (hb_b,), (hblocb,), hb_b)
                  rec.collective(hbloc.ap().opt(), hball.ap().opt(), (hblocb,), (hballb,), cch, GROUPS)
                  rec.dma("sp", HBL[0], hball[0:128, 16:32], (hballb,), (HBL[1],), HBL[1])
                  rec.dma("sp", HBR[0], hball[128:256, 0:16], (hballb,), (HBR[1],), HBR[1])
              norm_to_HT(cur, cur_b, 4 + l, early=early_halo, preloaded=True, sources=nsrc)
              self.dve(lambda e, HBL=HBL: e.tensor_scalar(HT[:, :, 0], HBL[0], hm[:, 0:1], None, op0=ALU.mult),
                       (HBL[1], hm_b), (HT_b,))
              self.dve(lambda e, HBR=HBR: e.tensor_scalar(HT[:, :, TE - 1], HBR[0], hm[:, 1:2], None, op0=ALU.mult),
                       (HBR[1], hm_b), (HT_b,))
              G, G_b = self.SB("G", B0 + 32832, [128, 44, T], BF16)
              GB = [self.SB("Gk%d" % jj, B0 + 32832 + 2048 * jj, [128, T], BF16)[1] for jj in range(44)]
              WUP = [self.SB("wup%d" % i, B0 + 122944 + 4096 * i, [128, KD, 128], BF16) for i in range(5)]
              ASB = [self.SB("asb%d" % i, B0 + 143424 + 4128 * i, [128, TE], F32) for i in range(2)]
              CB = [self.SB("cb%d" % i, B0 + 151680 + 4096 * i, [128, T], F32) for i in range(3)]
              WDN0 = self.SB("wdn0", B0 + 163968, [128, 44, 128], BF16)
              WDN1 = self.SB("wdn1", B0 + 122944, [128, 44, 128], BF16)
              XCf = [self.SB("xc%d" % i, ARENA - 8192 - 8192 + 4096 * i, [128, T], F32) for i in range(2)]
              XCf += [self.SB("xcf%d" % i, B0 + 143424 + 4096 * i, [128, T], F32) for i in range(2)]
              wu = wup_d[l]
              chk(8)
              for u in range(5):
                  rec.dma("pool", WUP[u][0], wu[u, :, :, :], (), (WUP[u][1],), WUP[u][1])
              gate_c = None
              for u in range(NU):
                  w, w_b = WUP[u % 5]
                  bs = (0, 1, 2) if u % 2 == 0 else (3, 4, 5)
                  for k in range(KD):
                      for tg in range(3):
                          self.mm(self.PS(bs[tg], 342), w[:, k, :], HT[:, k, tg * 342:(tg + 1) * 342], k == 0, k == KD - 1,
                                  (w_b, HT_b), (self.bank[bs[tg]],))
                  if u + 5 < NU:
                      rec.dma("pool", w, wu[u + 5, :, :, :], (), (w_b,), w_b)
                  if u == 40:
                      rec.dma("pool", WDN0[0], wdn_d[l][0, :, :, :], (), (WDN0[1],), WDN0[1])
                  a, a_b = ASB[u % 2]
                  for tg in range(3):
                      self.act(a[:, tg * 342:(tg + 1) * 342], self.PS(bs[tg], 342), AF.Copy, (self.bank[bs[tg]],), (a_b,))
                  c, c_b = CB[u % 3]
                  cp = [convp[:, l, u, q:q + 1] for q in range(4)]
                  self.dve(lambda e, c=c, a=a, cp=cp: e.tensor_scalar(c, a[:, 1:1 + T], cp[1], cp[3], op0=ALU.mult,
                                                                       op1=ALU.add), (a_b, convp_b), (c_b,))
                  self.dve(lambda e, c=c, a=a, cp=cp: e.scalar_tensor_tensor(out=c, in0=a[:, 0:T], scalar=cp[0], in1=c,
                                                                              op0=ALU.mult, op1=ALU.add),
                           (a_b, c_b, convp_b), (c_b,))
                  self.dve(lambda e, c=c, a=a, cp=cp: e.scalar_tensor_tensor(out=c, in0=a[:, 2:2 + T], scalar=cp[2], in1=c,
                                                                              op0=ALU.mult, op1=ALU.add),
                           (a_b, c_b, convp_b), (c_b,))
                  if u % 2 == 0:
                      self.act(c, c, AF.Gelu, (c_b,), (c_b,))
                      gate_c = (c, c_b)
                  else:
                      gc, gc_b = gate_c
                      self.dve(lambda e, gc=gc, c=c, jj=u // 2, G=G: e.tensor_tensor(out=G[:, jj, :], in0=gc, in1=c, op=ALU.mult),
                               (gc_b, c_b), (GB[u // 2],))
              msrc = proj_residual(44, (lambda k, hf, G=G: G[:, k, hf * 512:(hf + 1) * 512]), (lambda k, GB=GB: GB[k]), wdn_d[l], [WDN0, WDN1],
                                   XCf, cur, cur_b, xd, self.xd_b, [(6, 7), (2, 3), (4, 5)], stats=True, preloaded=1,
                                   xs=XSM)

        except _Stop:
            pass

        yb = [Buf("y%d" % k) for k in range(KD)]
        if self.final_norm:
            XCy = [self.SB("xc%d" % i, ARENA - 8192 - 8192 + 4096 * i, [128, T], F32) for i in range(2)]

            def o(k, xs, xs_b, g):
                xc, xc_b = XCy[k % 2]
                self.dve(lambda e, xs=xs, g=g, xc=xc: e.scalar_tensor_tensor(out=xc, in0=xs, scalar=g, in1=RSTD,
                                                                             op0=ALU.mult, op1=ALU.mult),
                         (xs_b, RSTD_b, gains_b), (xc_b,))
                rec.dma("sp", y_out[k, :, :], xc, (xc_b,), (yb[k],), xc_b)
            norm(cur, cur_b, 12, o, preloaded=(msrc is not None), sources=msrc)
        else:
            XCy = [self.SB("xc%d" % i, ARENA - 8192 - 8192 + 4096 * i, [128, T], F32) for i in range(2)]
            for k in range(KD):
                xc, xc_b = XCy[k % 2]
                rec.dma("sp", xc, cur[k, :, :], (cur_b[k],), (xc_b,), xc_b)
                rec.dma("sp", y_out[k, :, :], xc, (xc_b,), (yb[k],), xc_b)
        rec.finish()
        rec.finalize()
        with nc.Block() as block:
            @block.tensor
            def _(e):
                rec.emit("pe", e)

            @block.scalar
            def _(e):
                rec.emit("act", e)

            @block.vector
            def _(e):
                rec.emit("dve", e)

            @block.gpsimd
            def _(e):
                rec.emit("pool", e)

            @block.sync
            def _(e):
                rec.emit("sp", e)
        self.stack.close()
        return nc


def _su(W, cols):
    Wc = W[:, cols]
    n = Wc.shape[1] // 512
    return np.ascontiguousarray(Wc.reshape(KD, 128, n, 512).transpose(2, 1, 0, 3))


def _masks(c):
    k = np.arange(128)[:, None]
    q = np.arange(128)[None, :]
    out = np.zeros((128, 2, 3, 3, 128), np.float32)
    for gi, d in enumerate((1, 4)):
        for v in range(3):
            for di in range(3):
                dj = 128 * (di - 1) + k - q
                ok = np.abs(dj) <= 64
                if v == 0 and di == 0 and c == 0:
                    ok = np.zeros_like(ok)
                if v == 2 and di == 2 and c == 1:
                    ok = np.zeros_like(ok)
                out[:, gi, v, di, :] = np.where(ok, d * np.abs(dj), BIG)
    dt01 = out.reshape(128, 2 * 3 * 384)
    kk = np.arange(128)[:, None]
    qq = np.arange(64)[None, :]
    dj = kk - (64 * c + qq)
    dt2 = np.where(np.abs(dj) <= 64, 16.0 * np.abs(dj), BIG).astype(np.float32)
    hm = np.zeros((128, 2), np.float32)
    hm[:, 0] = float(c)
    hm[:, 1] = float(1 - c)
    return dt01, dt2, hm


def _prep_shared(inp, layers):
    f = lambda a: np.asarray(a, dtype=np.float32)
    sh = {}
    g = np.concatenate([f(inp["mix_norm_g"]), f(inp["ffn_norm_g"]), f(inp["mem_norm_g"]),
                        f(inp["final_norm_g"])[None, :]], axis=0)
    sh["gains"] = np.ascontiguousarray(g.reshape(13, KD, 128).transpose(2, 0, 1).reshape(128, 13 * KD))
    cw, cb = f(inp["ffn_conv_w"]), f(inp["ffn_conv_b"])
    cp = np.concatenate([cw, cb[:, None, :]], axis=1)
    cp = cp.reshape(4, 4, 2, 44, 128)
    sh["convp"] = np.ascontiguousarray(cp.transpose(4, 0, 3, 2, 1).reshape(128, 4 * NU * 4))
    for l in layers:
        j = l // 2
        wkv = f(inp["w_mem_kv"][l])
        if l % 2 == 0:
            W = f(inp["a_w_in"][j])
            cols = np.concatenate([np.arange(1536, 3072), np.arange(3072, 4608), np.arange(0, 1536),
                                   np.arange(4608, 5120)])
            Wo = f(inp["a_w_out"][j])
            nk = 8
        else:
            W = f(inp["b_w_in"][j])
            cols = np.concatenate([np.arange(1536, 3072), np.arange(0, 1536), np.arange(3072, 3584)])
            Wo = f(inp["b_w_out"][j])
            nk = 16
            sh["gvb%d" % l] = np.ascontiguousarray(np.broadcast_to(f(inp["b_v_norm_g"][j])[None, :], (128, 1536)))
            sb = f(inp["b_s_bias"][j])
            sh["sbb%d" % l] = np.ascontiguousarray(np.broadcast_to(sb.reshape(1, 1536), (128, 1536)))
            ws = f(inp["b_w_s"][j])
            sh["wst%d" % l] = np.ascontiguousarray(ws.transpose(2, 0, 1).reshape(128, 1536))
        sh["win%d" % l] = np.concatenate([_su(W, cols), _su(wkv, np.arange(1024))], axis=0)
        sh["wout%d" % l] = np.ascontiguousarray(Wo.reshape(nk, 128, 16, 128).transpose(2, 1, 0, 3))
        Wu = f(inp["ffn_w_up"][l])
        sh["wup%d" % l] = np.ascontiguousarray(Wu.reshape(KD, 128, 2, 44, 128).transpose(3, 2, 1, 0, 4).reshape(NU, 128, KD, 128))
        Wd = f(inp["ffn_w_down"][l])
        sh["wdn%d" % l] = np.ascontiguousarray(Wd.reshape(44, 128, 16, 128).transpose(2, 1, 0, 3))
    return sh


def _run(inp, layers, final_norm, x_shards):
    b = Builder(layers, final_norm)
    nc = b.build()
    sh = _prep_shared(inp, layers)
    mem = np.asarray(inp["mem"], dtype=np.float32)
    in_maps = []
    for core in range(8):
        bi, c = core // 2, core % 2
        m = dict(sh)
        m["xT"] = x_shards[core]
        m["memT"] = np.ascontiguousarray(mem[bi].T.reshape(KD, 128, 256))
        m["dt01"], m["dt2"], m["hm"] = _masks(c)
        in_maps.append(m)
    res = run_bass_kernel_spmd(nc, in_maps, core_ids=list(range(8)))
    key = "yT" if final_norm else "xo"
    return [np.asarray(r[key]) for r in res.results]


def kernel(**inputs):
    x = np.asarray(inputs["x"], dtype=np.float32)
    shards = []
    for core in range(8):
        bi, c = core // 2, core % 2
        shards.append(np.ascontiguousarray(x[bi, c * T:(c + 1) * T, :].T.reshape(KD, 128, T)))
    outs = _run(inputs, list(range(N_LAYERS)), True, shards)
    y = np.zeros((4, 2048, D), np.float32)
    for core in range(8):
        bi, c = core // 2, core % 2
        y[bi, c * T:(c + 1) * T, :] = outs[core].reshape(D, T).T
    return y
```

```python
import os
import numpy as np
import concourse.bass as bass
import concourse.mybir as mybir
from concourse.bass_utils import run_bass_kernel_spmd

F32 = mybir.dt.float32
BF16 = mybir.dt.bfloat16
U8 = mybir.dt.uint8
AF = mybir.ActivationFunctionType
ALU = mybir.AluOpType

D = 2048
T = 1024
TE = T + 2
KD = 16
FF = 5632
NU = 88
EPS = 1e-6
SCALE = 128 ** -0.5
BIG = 1.0e6
N_LAYERS = 4
ARENA = 212000
PERS = 16384


class Buf:
    __slots__ = ("name", "lw", "rd", "rdd", "dsem", "dcount", "lo", "hi", "ov")

    def __init__(self, name, lo=None, hi=None):
        self.name = name
        self.lw = []
        self.rd = {}
        self.rdd = []
        self.dsem = None
        self.dcount = 0
        self.lo, self.hi = lo, hi
        self.ov = [self]


class Op:
    __slots__ = ("eng", "fn", "is_dma", "deps", "needs_inc", "inc_idx", "pos", "dsem", "dval", "dinc")

    def __init__(self, eng, fn, is_dma):
        self.eng, self.fn, self.is_dma = eng, fn, is_dma
        self.deps = []
        self.needs_inc = False
        self.inc_idx = 0
        self.pos = 0
        self.dsem = None
        self.dval = 0
        self.dinc = 16


class Rec:
    ENGS = ("pe", "act", "dve", "pool", "sp")

    def __init__(self, nc, stack):
        self.nc = nc
        self.stack = stack
        self.q = {e: [] for e in self.ENGS}
        self.esem = {e: stack.enter_context(nc.semaphore("es_" + e)) for e in self.ENGS}
        self.sbufs = []
        self.nsem = 0
        self.all_dma_bufs = []

    def sbuf(self, name, lo, hi):
        b = Buf(name, lo, hi)
        for o in self.sbufs:
            if o.lo < hi and lo < o.hi:
                o.ov.append(b)
                b.ov.append(o)
        self.sbufs.append(b)
        return b

    def _dsem(self, b):
        if b.dsem is None:
            b.dsem = self.stack.enter_context(self.nc.semaphore("ds%d" % self.nsem))
            self.nsem += 1
            self.all_dma_bufs.append(b)
        return b.dsem

    def _deps(self, op, reads, writes):
        cand = []
        for b in reads:
            for o in b.ov:
                cand += o.lw
        for b in writes:
            for o in b.ov:
                cand += o.lw
                cand += list(o.rd.values())
                cand += o.rdd
        seen = set()
        for p in cand:
            if p is op or id(p) in seen:
                continue
            seen.add(id(p))
            if (not p.is_dma) and p.eng == op.eng and not op.is_dma:
                if op.eng == "pe" or op.pos - p.pos > 3:
                    continue
            op.deps.append(p)
            if not p.is_dma:
                p.needs_inc = True
        for b in reads:
            if op.is_dma:
                b.rdd.append(op)
                if len(b.rdd) > 8:
                    b.rdd = b.rdd[-8:]
            else:
                b.rd[op.eng] = op
        for b in writes:
            b.lw = [op]
            b.rd = {}
            b.rdd = []

    def compute(self, eng, fn, reads=(), writes=()):
        op = Op(eng, fn, False)
        op.pos = len(self.q[eng])
        self._deps(op, reads, writes)
        self.q[eng].append(op)
        return op

    def dma(self, q, out_ap, in_ap, reads, writes, sem_buf):
        op = Op(q, (lambda e, o=out_ap, i=in_ap: e.dma_start(out=o, in_=i)), True)
        op.pos = len(self.q[q])
        op.dsem = self._dsem(sem_buf)
        sem_buf.dcount += 16
        op.dval = sem_buf.dcount
        self._deps(op, reads, writes)
        self.q[q].append(op)
        return op

    def collective(self, in_ap, out_ap, reads, writes, sem_buf, groups):
        def fn(e, i=in_ap, o=out_ap):
            return e.collective_compute("AllGather", ALU.bypass, replica_groups=groups, ins=[i], outs=[o])
        op = Op("pool", fn, True)
        op.dinc = 1
        op.pos = len(self.q["pool"])
        op.dsem = self._dsem(sem_buf)
        sem_buf.dcount += 1
        op.dval = sem_buf.dcount
        self._deps(op, reads, writes)
        self.q["pool"].append(op)
        return op

    def finish(self):
        op = Op("sp", None, True)
        for b in self.all_dma_bufs:
            fake = Op("sp", None, True)
            fake.dsem, fake.dval = b.dsem, b.dcount
            op.deps.append(fake)
        self.q["sp"].append(op)

    def finalize(self):
        for e in self.ENGS:
            c = 0
            for op in self.q[e]:
                if (not op.is_dma) and op.needs_inc:
                    c += 1
                    op.inc_idx = c

    def emit(self, eng, e):
        obs = {}
        for op in self.q[eng]:
            for p in op.deps:
                if p.is_dma:
                    key, val = p.dsem, p.dval
                else:
                    key, val = self.esem[p.eng], p.inc_idx
                if obs.get(key, 0) < val:
                    e.wait_ge(key, val)
                    obs[key] = val
            if op.fn is None:
                continue
            ins = op.fn(e)
            if op.is_dma:
                ins.then_inc(op.dsem, op.dinc)
            elif op.needs_inc:
                ins.then_inc(self.esem[eng], 1)


class _Stop(Exception):
    pass


class Builder:
    def __init__(self, layers, final_norm, n_a_su=12, n_b_su=9):
        import contextlib
        self.layers = layers
        self.stack = contextlib.ExitStack()
        nc = self.nc = bass.Bass("TRN2", target_bir_lowering=False)
        self.rec = Rec(nc, self.stack)
        self.arena = nc.alloc_sbuf_tensor("arena", [128, ARENA], U8)
        self.psum = nc.alloc_psum_tensor("ps", [128, 4096], F32)
        self.bank = [Buf("bank%d" % i) for i in range(8)]
        self.dram = {}
        self._sbc = {}
        self.final_norm = final_norm

    def V(self, off, shape, dt):
        sz = 4 if dt == F32 else 2
        n = int(np.prod(shape[1:]))
        ap = self.arena[:, off:off + n * sz].bitcast(dt)
        if len(shape) == 3:
            ap = ap.rearrange("p (a b) -> p a b", b=shape[2])
        elif len(shape) == 4:
            ap = ap.rearrange("p (a b c) -> p a b c", b=shape[2], c=shape[3])
        return ap

    def SB(self, name, off, shape, dt):
        sz = 4 if dt == F32 else 2
        n = int(np.prod(shape[1:]))
        key = (name, off, n * sz)
        if key not in self._sbc:
            assert off + n * sz <= ARENA, (name, off, n * sz)
            self._sbc[key] = self.rec.sbuf(name, off, off + n * sz)
        return self.V(off, shape, dt), self._sbc[key]

    def PS(self, b, n=512, off=0):
        return self.psum[:, b * 512 + off: b * 512 + off + n]

    def din(self, name, shape, dt=F32):
        t = self.nc.dram_tensor(name, list(shape), dt, kind="ExternalInput")
        self.dram[name] = t
        return t

    def mm(self, out, lhsT, rhs, start, stop, reads, writes, skip=False):
        self.rec.compute("pe", (lambda e, o=out, l=lhsT, r=rhs, s=start, st=stop, sk=skip:
                                e.matmul(o, l, r, start=s, stop=st, skip_group_check=sk)), reads, writes)

    def act(self, out, in_, func, reads, writes, scale=1.0, bias=0.0, accum=None):
        if accum is None:
            fn = (lambda e, o=out, i=in_, f=func, s=scale, b=bias: e.activation(o, i, f, bias=b, scale=s))
        else:
            fn = (lambda e, o=out, i=in_, f=func, s=scale, b=bias, a=accum:
                  e.activation(o, i, f, bias=b, scale=s, accum_out=a))
        self.rec.compute("act", fn, reads, writes)

    def dve(self, fn, reads, writes):
        self.rec.compute("dve", fn, reads, writes)

    def build(self):
        nc, rec = self.nc, self.rec
        L = self.layers
        x_in = self.din("xT", [KD, 128, T])
        memT = self.din("memT", [KD, 128, 256])
        gains_d = self.din("gains", [128, 13 * 16])
        convp_d = self.din("convp", [128, 4 * NU * 4])
        dt01_d = self.din("dt01", [128, 2 * 3 * 384])
        dt2_d = self.din("dt2", [128, 64])
        hm_d = self.din("hm", [128, 2])
        win_d, wout_d, wup_d, wdn_d, bc_d = {}, {}, {}, {}, {}
        for l in L:
            isA = (l % 2 == 0)
            win_d[l] = self.din("win%d" % l, [12 if isA else 9, 128, KD, 512])
            wout_d[l] = self.din("wout%d" % l, [16, 128, 8 if isA else 16, 128])
            wup_d[l] = self.din("wup%d" % l, [NU, 128, KD, 128])
            wdn_d[l] = self.din("wdn%d" % l, [16, 128, 44, 128])
            if not isA:
                bc_d[l] = (self.din("gvb%d" % l, [128, 1536]), self.din("sbb%d" % l, [128, 1536]),
                           self.din("wst%d" % l, [128, 1536]))
        if self.final_norm:
            y_out = nc.dram_tensor("yT", [KD, 128, T], F32, kind="ExternalOutput")
        else:
            y_out = nc.dram_tensor("xo", [KD, 128, T], F32, kind="ExternalOutput")
        xd = nc.dram_tensor("xd", [KD, 128, T], F32)
        kloc = [nc.dram_tensor("kloc%d" % g, [512, T], BF16) for g in range(3)]
        vloc = [nc.dram_tensor("vloc%d" % g, [T, 512], BF16) for g in range(3)]
        kall = [nc.dram_tensor("kall%d" % g, [1024, T], BF16) for g in range(3)]
        vall = [nc.dram_tensor("vall%d" % g, [2 * T, 512], BF16) for g in range(3)]
        hbloc = nc.dram_tensor("hbloc", [128, 32], BF16)
        hball = nc.dram_tensor("hball", [256, 32], BF16)
        self.xd_b = [Buf("xd%d" % k) for k in range(KD)]
        self.xin_b = [Buf("xin%d" % k) for k in range(KD)]
        klocb = [Buf("kloc%d" % i) for i in range(3)]
        vlocb = [Buf("vloc%d" % i) for i in range(3)]
        kallb = [Buf("kall%d" % g) for g in range(3)]
        vallb = [Buf("vall%d" % g) for g in range(3)]
        hblocb, hballb = Buf("hbloc"), Buf("hball")
        cck, ccv, cch = [Buf("cck%d" % g) for g in range(3)], [Buf("ccv%d" % g) for g in range(3)], Buf("cch")
        GROUPS = [[0, 1], [2, 3], [4, 5], [6, 7]]

        ones, ones_b = self.SB("ones", 0, [128, 128], BF16)
        gains, gains_b = self.SB("gains", 256, [128, 208], F32)
        convp, convp_b = self.SB("convp", 1088, [128, 4, NU, 4], F32)
        dt01, dt01_b = self.SB("dt01", 6720, [128, 6, 384], F32)
        dt2, dt2_b = self.SB("dt2", 15936, [128, 64], F32)
        hm, hm_b = self.SB("hm", 16192, [128, 2], F32)
        rec.compute("dve", lambda e: e.memset(ones, 1.0), (), (ones_b,))
        EPSB, epsb_b = self.SB("epsb", 16224, [128, 1], F32)
        rec.compute("dve", lambda e: e.memset(EPSB, EPS), (), (epsb_b,))
        rec.dma("sp", gains, gains_d[:, :], (), (gains_b,), gains_b)
        rec.dma("sp", self.V(1088, [128, 4 * NU * 4], F32), convp_d[:, :], (), (convp_b,), convp_b)
        rec.dma("sp", self.V(6720, [128, 2304], F32), dt01_d[:, :], (), (dt01_b,), dt01_b)
        rec.dma("sp", dt2, dt2_d[:, :], (), (dt2_b,), dt2_b)
        rec.dma("sp", hm, hm_d[:, :], (), (hm_b,), hm_b)
        self.ones, self.ones_b = ones, ones_b
        self.gains, self.gains_b = gains, gains_b

        B0 = PERS
        HT, HT_b = self.SB("HT", B0, [128, KD, TE], BF16)
        HTK = [self.rec.sbuf("HTk%d" % k, B0 + 2052 * k, B0 + 2052 * (k + 1)) for k in range(KD)]
        POOL_K = ()
        NRM = ARENA - 8192
        SQ = [self.SB("sq%d" % i, NRM + 2048 * i, [128, T], BF16) for i in range(2)]
        RSTD, RSTD_b = self.SB("rstd", NRM + 4096, [128, T], F32)
        XSo = B0 + 32832
        XS = [self.SB("xs%d" % k, XSo + 4096 * k, [128, T], F32) for k in range(KD)]

        XSM = [self.SB("xsm%d" % k, B0 + 81984 + 4096 * k, [128, T], F32) for k in range(8)] + \
              [XS[k] for k in range(8, KD)]

        def norm_load(src, src_b, k, xs=None):
            xs = XS if xs is None else xs
            rec.dma("sp", xs[k][0], src[k, :, :], (src_b[k],), (xs[k][1],), xs[k][1])

        def norm_stats(k):
            sq, sq_b = SQ[k % 2]
            self.act(sq, XS[k][0], AF.Square, (XS[k][1],), (sq_b,))
            for hf in range(2):
                self.mm(self.PS(hf), ones, sq[:, hf * 512:(hf + 1) * 512], k == 0, k == KD - 1,
                        (ones_b, sq_b), (self.bank[hf],))

        def norm_stats_from(k, x_ap, x_b):
            sq, sq_b = SQ[k % 2]
            self.act(sq, x_ap, AF.Square, (x_b,), (sq_b,))

            def pe_part(k=k, sq=sq, sq_b=sq_b):
                for hf in range(2):
                    self.mm(self.PS(hf), ones, sq[:, hf * 512:(hf + 1) * 512], k == 0, k == KD - 1,
                            (ones_b, sq_b), (self.bank[hf],))
            return pe_part

        def norm(src, src_b, gcol, out_fn, early=None, preloaded=False, sources=None):
            if sources is None:
                sources = [XS[k] for k in range(KD)]
            if not preloaded:
                for k in range(KD):
                    norm_load(src, src_b, k)
                for k in range(KD):
                    norm_stats(k)
            tmp = self.psum[:, 0:1024]
            self.act(RSTD, tmp, AF.Ln, (self.bank[0], self.bank[1]), (RSTD_b,), scale=1.0 / D, bias=EPSB)
            self.act(RSTD, RSTD, AF.Exp, (RSTD_b,), (RSTD_b,), scale=-0.5)
            if early is not None:
                early(sources)
            for k in [12, 13, 14, 15] + list(range(12)):
                out_fn(k, sources[k][0], sources[k][1], gains[:, gcol * 16 + k: gcol * 16 + k + 1])

        def norm_to_HT(src, src_b, gcol, early=None, preloaded=False, sources=None):
            def o(k, xs, xs_b, g):
                eng = "pool" if k in POOL_K else "dve"
                rec.compute(eng, lambda e, k=k, xs=xs, g=g: e.scalar_tensor_tensor(
                    out=HT[:, k, 1:1 + T], in0=xs, scalar=g, in1=RSTD, op0=ALU.mult, op1=ALU.mult),
                    (xs_b, RSTD_b, gains_b), (HTK[k],))
            norm(src, src_b, gcol, o, early, preloaded, sources)

        def proj_residual(nk, rhs, rhs_b, w_d, wslots, xc, src, src_b, dst, dst_b, bank_sets,
                          stats=False, prefetch=False, preloaded=0, xs=None):
            nw = len(wslots)
            nx = len(xc)
            assert nx == 4
            for m in range(preloaded, min(nw, 16)):
                rec.dma("pool", wslots[m][0], w_d[m, :, :, :], (), (wslots[m][1],), wslots[m][1])

            def loadx(m):
                x, x_b = xc[m % nx]
                rec.dma("sp", x, src[m, :, :], (src_b[m],), (x_b,), x_b)
            for m in range(nx - 2):
                loadx(m)
            pend = []
            for m in range(16):
                if m + nx - 2 < 16:
                    loadx(m + nx - 2)
                if prefetch and m >= 2 and m - 2 < 12:
                    norm_load(dst, dst_b, m - 2)
                while pend and pend[0][0] <= m - 2:
                    pend.pop(0)[1]()
                w, w_b = wslots[m % nw]
                bs = bank_sets[m % len(bank_sets)]
                x, x_b = xc[m % nx]
                for k in range(nk):
                    for hf in range(2):
                        self.mm(self.PS(bs[hf]), w[:, k, :], rhs(k, hf), k == 0, k == nk - 1,
                                (w_b, rhs_b(k) if callable(rhs_b) else rhs_b), (self.bank[bs[hf]],))
                if m + nw < 16:
                    rec.dma("pool", w, w_d[m + nw, :, :, :], (), (w_b,), w_b)
                for hf in range(2):
                    self.dve(lambda e, x=x, hf=hf, b=bs[hf]: e.tensor_tensor(
                        out=x[:, hf * 512:(hf + 1) * 512], in0=self.PS(b), in1=x[:, hf * 512:(hf + 1) * 512],
                        op=ALU.add), (self.bank[bs[hf]], x_b), (x_b,))
                if stats:
                    pend.append((m, norm_stats_from(m, x, x_b)))
                rec.dma("sp", dst[m, :, :], x, (x_b,), (dst_b[m],), x_b)
            while pend:
                pend.pop(0)[1]()
            xs = XS if xs is None else xs
            if stats and not prefetch:
                for k in range(12):
                    norm_load(dst, dst_b, k, xs)
            return [xs[k] for k in range(12)] + [xc[k % nx] for k in range(12, 16)]

        def mem_load(ms_off):
            ms, ms_b = self.SB("memstage%d" % ms_off, ms_off, [128, KD, 256], F32)
            rec.dma("sp", ms, memT.ap().rearrange("k p t -> p k t"), (), (ms_b,), ms_b)

        def mem_norm(l, ms_off, MEMN):
            memn, memn_b = MEMN
            ms, ms_b = self.SB("memstage%d" % ms_off, ms_off, [128, KD, 256], F32)
            for k in range(KD):
                sq, sq_b = SQ[k % 2]
                self.act(sq[:, 0:256], ms[:, k, :], AF.Square, (ms_b,), (sq_b,))
                self.mm(self.PS(6, 256), ones, sq[:, 0:256], k == 0, k == KD - 1, (ones_b, sq_b), (self.bank[6],))
            rs = RSTD[:, 0:256]
            self.act(rs, self.PS(6, 256), AF.Ln, (self.bank[6],), (RSTD_b,), scale=1.0 / D, bias=EPSB)
            self.act(rs, rs, AF.Exp, (RSTD_b,), (RSTD_b,), scale=-0.5)
            for k in range(KD):
                g = gains[:, (8 + l) * 16 + k: (8 + l) * 16 + k + 1]
                self.dve(lambda e, k=k, g=g, ms=ms, memn=memn: e.scalar_tensor_tensor(
                    out=memn[:, k, :], in0=ms[:, k, :], scalar=g, in1=rs, op0=ALU.mult, op1=ALU.mult),
                    (ms_b, RSTD_b, gains_b), (memn_b,))

        def mem_prepare(l, w_d, su0, WIN, MEMN, KMT, VM, XCm):
            memn, memn_b = MEMN
            kmt, kmt_b = KMT
            vm, vm_b = VM
            w, w_b = WIN[su0 % 2]
            for hh in range(4):
                for k in range(KD):
                    self.mm(self.PS(6, 256), w[:, k, hh * 128:(hh + 1) * 128], memn[:, k, :], k == 0, k == KD - 1,
                            (w_b, memn_b), (self.bank[6],))
                self.act(kmt[:, hh, :], self.PS(6, 256), AF.Copy, (self.bank[6],), (kmt_b,))
            w, w_b = WIN[(su0 + 1) % 2]
            for tt in range(2):
                for k in range(KD):
                    self.mm(self.PS(7), memn[:, k, tt * 128:(tt + 1) * 128], w[:, k, :], k == 0, k == KD - 1,
                            (w_b, memn_b), (self.bank[7],))
                self.act(vm[:, tt, :], self.PS(7), AF.Copy, (self.bank[7],), (vm_b,))

        def mem_attention(QMT, KMT, VM, CAT, c0, PT, RC):
            qmt, qmt_b = QMT
            kmt, kmt_b = KMT
            vm, vm_b = VM
            cat, cat_b = CAT
            rc, rc_b = RC
            for hh in range(4):
                items = [(hf, kt) for hf in range(2) for kt in range(2)]
                sb_of = lambda i: 4 + (i % 3)

                def s_mm(i):
                    hf, kt = items[i]
                    self.mm(self.PS(sb_of(i)), kmt[:, hh, kt * 128:(kt + 1) * 128],
                            qmt[:, hh, hf * 512:(hf + 1) * 512], True, True, (kmt_b, qmt_b), (self.bank[sb_of(i)],))
                s_mm(0)
                for i in range(4):
                    if i + 1 < 4:
                        s_mm(i + 1)
                    hf, kt = items[i]
                    pt, pt_b = PT[i % len(PT)]
                    self.act(pt, self.PS(sb_of(i)), AF.Exp, (self.bank[sb_of(i)],), (pt_b,), scale=SCALE)
                    self.mm(self.PS(hf), vm[:, kt, hh * 128:(hh + 1) * 128], pt, kt == 0, kt == 1,
                            (vm_b, pt_b), (self.bank[hf],))
                    self.mm(self.PS(2 + hf), ones, pt, kt == 0, kt == 1, (ones_b, pt_b), (self.bank[2 + hf],))
                self.act(rc, self.psum[:, 1024:2048], AF.Ln, (self.bank[2], self.bank[3]), (rc_b,))
                self.act(rc, rc, AF.Exp, (rc_b,), (rc_b,), scale=-1.0)
                self.dve(lambda e, hh=hh: e.tensor_tensor(out=cat[:, c0 + hh, :], in0=self.psum[:, 0:1024], in1=rc,
                                                          op=ALU.mult), (self.bank[0], self.bank[1], rc_b), (cat_b,))

        def su_fm(w, w_b, evac):
            for sub in range(4):
                bs = (4, 5) if sub % 2 == 0 else (6, 7)
                for k in range(KD):
                    for hf in range(2):
                        self.mm(self.PS(bs[hf]), w[:, k, sub * 128:(sub + 1) * 128],
                                HT[:, k, 1 + hf * 512: 1 + (hf + 1) * 512], k == 0, k == KD - 1,
                                (w_b, HT_b), (self.bank[bs[hf]],))
                evac(sub, self.psum[:, bs[0] * 512: bs[0] * 512 + 1024], (self.bank[bs[0]], self.bank[bs[1]]))

        def su_tm(w, w_b, evac):
            for tt in range(8):
                b = tt % 4
                for k in range(KD):
                    self.mm(self.PS(b), HT[:, k, 1 + tt * 128: 1 + (tt + 1) * 128], w[:, k, :], k == 0, k == KD - 1,
                            (w_b, HT_b), (self.bank[b],))
                evac(tt, self.PS(b), (self.bank[b],))

        cur, cur_b = x_in, self.xin_b
        msrc = None
        stop_at = int(os.environ.get("KSTOP", "99"))

        def chk(n):
            if n >= stop_at:
                raise _Stop()
        try:
          for l in L:
              isA = (l % 2 == 0)
              j = l // 2
              nsu = 12 if isA else 9
              wd = win_d[l]
              ms_off = B0 + 114752
              mem_load(ms_off)
              if msrc is None:
                  norm_to_HT(cur, cur_b, l)
              else:
                  norm_to_HT(cur, cur_b, l, preloaded=True, sources=msrc)
              MEMN0 = self.SB("MEMN", B0 + 163904, [128, KD, 256], BF16) if isA else \
                  self.SB("MEMNb", B0 + 147520, [128, KD, 256], BF16)
              mem_norm(l, ms_off, MEMN0)
              chk(1)
              WIN = [self.SB("win%d" % i, B0 + 32832 + 16384 * i, [128, KD, 512], BF16) for i in range(2)]
              su_next = [0]

              def su_load(s):
                  w, w_b = WIN[s % 2]
                  rec.dma("pool", w, wd[s, :, :, :], (), (w_b,), w_b)

              su_load(0)
              su_load(1)

              def su_done(s):
                  if s + 2 < nsu:
                      su_load(s + 2)

              XCm = [self.SB("xc%d" % i, ARENA - 8192 - 8192 + 4096 * i, [128, T], F32) for i in range(2)]
              if isA:
                  STG = [self.SB("stg%d" % i, B0 + 65600 + 8192 * i, [128, 4096], BF16) for i in range(4)]
                  QT = self.SB("QT", B0 + 98368, [128, 12, T], BF16)
                  QMT = self.SB("QMT", B0 + 122944, [128, 4, T], BF16)
                  CAT = self.SB("CAT", B0 + 131136, [128, 8, T], BF16)
                  MEMN = self.SB("MEMN", B0 + 163904, [128, KD, 256], BF16)
                  KMT = self.SB("KMT", B0 + 172096, [128, 4, 256], BF16)
                  VM = self.SB("VM", B0 + 174144, [128, 2, 512], BF16)
                  WK = B0 + 86016
                  SBW = [self.SB("sbw%d" % i, WK + 1536 * i, [128, 384], F32) for i in range(3)]
                  PT = [self.SB("pt%d" % i, WK + 4608 + 1024 * i, [128, 512], BF16) for i in range(3)]
                  RC = self.SB("rc", WK + 7680, [128, T], F32)
                  for s in range(3):
                      w, w_b = WIN[s % 2]
                      st, st_b = STG[s % 2]
                      st3 = st.rearrange("p (h t) -> p h t", t=T)

                      def ev(sub, ps, pb, st3=st3, st_b=st_b):
                          self.act(st3[:, sub, :], ps, AF.Copy, pb, (st_b,))
                      su_fm(w, w_b, ev)
                      su_done(s)
                      rec.dma("sp", kloc[s][:, :].rearrange("(h e) t -> e h t", e=128), st3,
                              (st_b,), (klocb[s],), st_b)
                      rec.collective(kloc[s].ap().opt(), kall[s].ap().opt(), (klocb[s],), (kallb[s],), cck[s], GROUPS)
                  chk(2)
                  for s in range(3, 6):
                      w, w_b = WIN[s % 2]
                      st, st_b = STG[2 + s % 2]
                      st3 = st.rearrange("p (t c) -> p t c", c=512)

                      def ev(tt, ps, pb, st3=st3, st_b=st_b):
                          self.act(st3[:, tt, :], ps, AF.Copy, pb, (st_b,))
                      su_tm(w, w_b, ev)
                      su_done(s)
                      g = s - 3
                      rec.dma("sp", vloc[g][:, :].rearrange("(t p) c -> p t c", p=128), st3,
                              (st_b,), (vlocb[g],), st_b)
                      rec.collective(vloc[g].ap().opt(), vall[g].ap().opt(), (vlocb[g],), (vallb[g],), ccv[g], GROUPS)
                  chk(3)
                  qt, qt_b = QT
                  for s in range(6, 9):
                      w, w_b = WIN[s % 2]

                      def ev(sub, ps, pb, g=s - 6):
                          self.act(qt[:, g * 4 + sub, :], ps, AF.Copy, pb, (qt_b,))
                      su_fm(w, w_b, ev)
                      su_done(s)
                  qmt, qmt_b = QMT
                  w, w_b = WIN[9 % 2]

                  def ev(sub, ps, pb):
                      self.act(qmt[:, sub, :], ps, AF.Copy, pb, (qmt_b,))
                  su_fm(w, w_b, ev)
                  su_done(9)
                  mem_prepare(l, wd, 10, WIN, MEMN, KMT, VM, XCm)
                  mem_attention(QMT, KMT, VM, CAT, 4, PT, RC)
                  chk(4)
                  K0 = self.SB("K0", B0 + 0, [128, 4, 1280], BF16)
                  K1 = self.SB("K1", B0 + 10240, [128, 4, 2048], BF16)
                  K2 = self.SB("K2", B0 + 26624, [128, 4, 2048], BF16)
                  V0 = self.SB("V0", B0 + 43008, [128, 10, 512], BF16)
                  V1 = self.SB("V1", B0 + 53248, [128, 4, 4, 512], BF16)
                  V2 = self.SB("V2", B0 + 69632, [128, 16, 512], BF16)
                  kl = lambda g, c0, c1: kloc[g][:, c0:c1].rearrange("(h e) t -> e h t", e=128)
                  ka = lambda R, g, c0, c1: kall[g][R * 512:(R + 1) * 512, c0:c1].rearrange("(h e) t -> e h t", e=128)
                  k0, k0b = K0
                  rec.dma("sp", k0[:, :, 128:1152], kl(0, 0, T), (klocb[0],), (k0b,), k0b)
                  rec.dma("sp", k0[:, :, 0:128], ka(0, 0, 896, 1024), (kallb[0],), (k0b,), k0b)
                  rec.dma("sp", k0[:, :, 1152:1280], ka(1, 0, 0, 128), (kallb[0],), (k0b,), k0b)
                  k1, k1b = K1
                  rec.dma("sp", k1[:, :, 512:1536], kl(1, 0, T), (klocb[1],), (k1b,), k1b)
                  rec.dma("sp", k1[:, :, 0:512], ka(0, 1, 512, 1024), (kallb[1],), (k1b,), k1b)
                  rec.dma("sp", k1[:, :, 1536:2048], ka(1, 1, 0, 512), (kallb[1],), (k1b,), k1b)
                  k2, k2b = K2
                  rec.dma("sp", k2[:, :, 0:1024], ka(0, 2, 0, T), (kallb[2],), (k2b,), k2b)
                  rec.dma("sp", k2[:, :, 1024:2048], ka(1, 2, 0, T), (kallb[2],), (k2b,), k2b)
                  v0, v0b = V0
                  rec.dma("sp", v0[:, 1:9, :], vloc[0][:, :].rearrange("(t p) c -> p t c", p=128), (vlocb[0],), (v0b,), v0b)
                  rec.dma("sp", v0[:, 0, :], vall[0][896:1024, :], (vallb[0],), (v0b,), v0b)
                  rec.dma("sp", v0[:, 9, :], vall[0][1024:1152, :], (vallb[0],), (v0b,), v0b)
                  v1, v1b = V1
                  for a_ in range(2):
                      rec.dma("sp", v1[:, 1 + a_, :, :],
                              vloc[1][a_ * 512:(a_ + 1) * 512, :].rearrange("(p r) c -> p r c", r=4),
                              (vlocb[1],), (v1b,), v1b)
                  rec.dma("sp", v1[:, 0, :, :], vall[1][512:1024, :].rearrange("(p r) c -> p r c", r=4),
                          (vallb[1],), (v1b,), v1b)
                  rec.dma("sp", v1[:, 3, :, :], vall[1][1024:1536, :].rearrange("(p r) c -> p r c", r=4),
                          (vallb[1],), (v1b,), v1b)
                  v2, v2b = V2
                  rec.dma("sp", v2[0:64, :, :], vall[2][0:1024, :].rearrange("(p r) c -> p r c", r=16),
                          (vallb[2],), (v2b,), v2b)
                  rec.dma("sp", v2[64:128, :, :], vall[2][1024:2048, :].rearrange("(p r) c -> p r c", r=16),
                          (vallb[2],), (v2b,), v2b)
                  cat, cat_b = CAT
                  rc, rc_b = RC
                  slopes = [2.0 ** (-8.0 * (i + 1) / 12) for i in range(12)]
                  chk(5)
                  for hh in range(4):
                      blocks = []
                      for b in range(8):
                          v = 0 if b == 0 else (2 if b == 7 else 1)
                          sm = [(k0[:, hh, (b + di) * 128:(b + di + 1) * 128], qt[:, hh, b * 128:(b + 1) * 128], di * 128, 128)
                                for di in range(3)]
                          pv = [(b // 4, (b % 4) * 128, 128, 1, v0[:, b + di, hh * 128:(hh + 1) * 128], di * 128, 128)
                                for di in range(3)]
                          blocks.append((sm, dt01[:, v, :], 384, pv, slopes[hh], (k0b,), (v0b,)))
                      for r in range(4):
                          for b in range(2):
                              v = 3 + (0 if b == 0 else 2)
                              sm = [(k1[:, hh, 512 * (b + di) + r: 512 * (b + di) + r + 509: 4],
                                     qt[:, 4 + hh, 512 * b + r: 512 * b + r + 509: 4], di * 128, 128) for di in range(3)]
                              pv = [(b, r, 128, 4, v1[:, b + di, r, hh * 128:(hh + 1) * 128], di * 128, 128)
                                    for di in range(3)]
                              blocks.append((sm, dt01[:, v, :], 384, pv, slopes[4 + hh], (k1b,), (v1b,)))
                      for r in range(16):
                          sm = [(k2[:, hh, r: r + 2033: 16], qt[:, 8 + hh, r: r + 1009: 16], 0, 64)]
                          pv = [(0, r, 32, 16, v2[:, r, hh * 128:(hh + 1) * 128], 0, 32),
                                (1, r, 32, 16, v2[:, r, hh * 128:(hh + 1) * 128], 32, 32)]
                          blocks.append((sm, dt2[:, :], 64, pv, slopes[8 + hh], (k2b,), (v2b,)))
                      touched = set()

                      def s_stage(i):
                          sm, dist, nc_, pv, slope, kb, vb = blocks[i]
                          bk = 4 + (i % 3)
                          for (lhsT, rhs, co, n) in sm:
                              self.mm(self.PS(bk, n, co), lhsT, rhs, True, True, kb + (qt_b,), (self.bank[bk],))
                      s_stage(0)
                      s_stage(1)
                      for i in range(len(blocks)):
                          if i + 2 < len(blocks):
                              s_stage(i + 2)
                          sm, dist, nc_, pv, slope, kb, vb = blocks[i]
                          bk = 4 + (i % 3)
                          sbw, sbw_b = SBW[i % 3]
                          pt, pt_b = PT[i % 3]
                          self.dve(lambda e, sbw=sbw, dist=dist, nc_=nc_, bk=bk, slope=slope: e.scalar_tensor_tensor(
                              out=sbw[:, 0:nc_], in0=dist, scalar=-slope / SCALE, in1=self.PS(bk, nc_),
                              op0=ALU.mult, op1=ALU.add), (self.bank[bk], dt01_b, dt2_b), (sbw_b,))
                          self.act(pt[:, 0:nc_], sbw[:, 0:nc_], AF.Exp, (sbw_b,), (pt_b,), scale=SCALE)
                          for (ob, c0, n, step, lhsT, pc, pn) in pv:
                              if step == 1:
                                  o_ap = self.PS(ob, n, c0)
                                  d_ap = self.PS(2 + ob, n, c0)
                              else:
                                  o_ap = self.psum[:, ob * 512 + c0: ob * 512 + c0 + (n - 1) * step + 1: step]
                                  d_ap = self.psum[:, (2 + ob) * 512 + c0: (2 + ob) * 512 + c0 + (n - 1) * step + 1: step]
                              first = ob not in touched
                              touched.add(ob)
                              self.mm(o_ap, lhsT, pt[:, pc:pc + pn], first, False, vb + (pt_b,), (self.bank[ob],), skip=True)
                              self.mm(d_ap, ones, pt[:, pc:pc + pn], first, False, (ones_b, pt_b), (self.bank[2 + ob],),
                                      skip=True)
                      self.act(rc, self.psum[:, 1024:2048], AF.Ln, (self.bank[2], self.bank[3]), (rc_b,))
                      self.act(rc, rc, AF.Exp, (rc_b,), (rc_b,), scale=-1.0)
                      self.dve(lambda e, hh=hh, cat=cat, rc=rc: e.tensor_tensor(
                          out=cat[:, hh, :], in0=self.psum[:, 0:1024], in1=rc, op=ALU.mult),
                          (self.bank[0], self.bank[1], rc_b), (cat_b,))
                  nk_out = 8
                  chk(6)
              else:
                  gvb_d, sbb_d, wst_d = bc_d[l]
                  VG = self.SB("VG", B0 + 65600, [128, 8, 1536], F32)
                  UT = self.SB("UT", B0 + 65600, [128, 12, T], F32)
                  VN = self.SB("VN", B0 + 114752, [128, 8, 1536], BF16)
                  QMT = self.SB("QMTb", B0 + 139328, [128, 4, T], BF16)
                  MEMN = self.SB("MEMNb", B0 + 147520, [128, KD, 256], BF16)
                  KMT = self.SB("KMTb", B0 + 155712, [128, 4, 256], BF16)
                  VM = self.SB("VMb", B0 + 157760, [128, 2, 512], BF16)
                  GVB = self.SB("GVB", B0 + 159808, [128, 1536], F32)
                  SBB = self.SB("SBB", B0 + 165952, [128, 12, 128], F32)
                  WST = self.SB("WST", B0 + 172096, [128, 12, 128], BF16)
                  WK = B0 + 175168
                  TMPB = [self.SB("tmpb%d" % i, WK + 2048 * i, [128, 512], F32) for i in range(2)]
                  PT = [self.SB("ptb%d" % i, WK + 4096 + 1024 * i, [128, 512], BF16) for i in range(3)]
                  RC = self.SB("rcb", WK + 7168, [128, T], F32)
                  SS = self.SB("ssb", WK + 11264, [128, 8], F32)
                  RS8 = self.SB("rs8", WK + 11296, [128, 8], F32)
                  CAT = self.SB("CATb", B0 + 0, [128, 16, T], BF16)
                  rec.dma("sp", GVB[0], gvb_d[:, :], (), (GVB[1],), GVB[1])
                  rec.dma("sp", self.V(B0 + 165952, [128, 1536], F32), sbb_d[:, :], (), (SBB[1],), SBB[1])
                  vg, vg_b = VG
                  vn, vn_b = VN
                  ss, ss_b = SS
                  rs8, rs8_b = RS8
                  self.dve(lambda e, ss=ss: e.memset(ss, 0.0), (), (ss_b,))
                  junk, junk_b = self.SB("junk", B0 + 139328, [128, 1536], BF16)
                  for s in range(3):
                      w, w_b = WIN[s % 2]

                      def ev(tt, ps, pb, s=s):
                          self.act(vg[:, tt, s * 512:(s + 1) * 512], ps, AF.Gelu, pb, (vg_b,))
                          if s == 2:
                              self.act(junk, vg[:, tt, :], AF.Square, (vg_b,), (junk_b, ss_b), accum=ss[:, tt:tt + 1])
                              self.act(rs8[:, tt:tt + 1], ss[:, tt:tt + 1], AF.Ln, (ss_b,), (rs8_b,), scale=1.0 / 1536,
                                       bias=EPSB)
                              self.act(rs8[:, tt:tt + 1], rs8[:, tt:tt + 1], AF.Exp, (rs8_b,), (rs8_b,), scale=-0.5)
                              self.dve(lambda e, tt=tt, vn=vn, vg=vg, rs8=rs8, GVB=GVB: e.scalar_tensor_tensor(
                                  out=vn[:, tt, :], in0=vg[:, tt, :], scalar=rs8[:, tt:tt + 1], in1=GVB[0],
                                  op0=ALU.mult, op1=ALU.mult), (vg_b, rs8_b, GVB[1]), (vn_b,))
                      su_tm(w, w_b, ev)
                      su_done(s)
                  ut, ut_b = UT
                  for s in range(3, 6):
                      w, w_b = WIN[s % 2]

                      def ev(sub, ps, pb, g=s - 3):
                          self.act(ut[:, g * 4 + sub, :], ps, AF.Gelu, pb, (ut_b,))
                      su_fm(w, w_b, ev)
                      su_done(s)
                  qmt, qmt_b = QMT
                  w, w_b = WIN[6 % 2]

                  def ev(sub, ps, pb):
                      self.act(qmt[:, sub, :], ps, AF.Copy, pb, (qmt_b,))
                  su_fm(w, w_b, ev)
                  su_done(6)
                  mem_prepare(l, wd, 7, WIN, MEMN, KMT, VM, XCm)
                  cat, cat_b = CAT
                  mem_attention(QMT, KMT, VM, CAT, 12, PT, RC)
                  rec.dma("pool", WST[0], wst_d[:, :].rearrange("q (g p) -> q g p", p=128), (), (WST[1],), WST[1])
                  wst, wst_b = WST
                  sbb, sbb_b = SBB
                  it = 0
                  for gg in range(12):
                      for cq in range(2):
                          bk = 4 + (it % 3)
                          tb, tb_b = TMPB[it % 2]
                          it += 1
                          for cc in range(4):
                              self.mm(self.PS(bk, 128, cc * 128), vn[:, cq * 4 + cc, gg * 128:(gg + 1) * 128], wst[:, gg, :],
                                      True, True, (vn_b, wst_b), (self.bank[bk],))
                          self.dve(lambda e, tb=tb, bk=bk, gg=gg, sbb=sbb: e.tensor_tensor(
                              out=tb.rearrange("p (c q) -> p c q", q=128),
                              in0=self.PS(bk).rearrange("p (c q) -> p c q", q=128),
                              in1=sbb[:, gg:gg + 1, :].to_broadcast([128, 4, 128]), op=ALU.add),
                              (self.bank[bk], sbb_b), (tb_b,))
                          self.dve(lambda e, tb=tb, gg=gg, cq=cq, cat=cat, ut=ut: e.tensor_tensor(
                              out=cat[:, gg, cq * 512:(cq + 1) * 512], in0=tb, in1=ut[:, gg, cq * 512:(cq + 1) * 512],
                              op=ALU.mult), (tb_b, ut_b), (cat_b,))
                  nk_out = 16
              cat, cat_b = CAT
              WO = [self.SB("woA%d" % i, B0 + 98368 + 4096 * i, [128, nk_out, 128], BF16) for i in range(4)] if isA else \
                   [self.SB("woB%d" % i, B0 + 114752 + 4096 * i, [128, nk_out, 128], BF16) for i in range(4)]
              XCo = [self.SB("xc%d" % i, ARENA - 8192 - 8192 + 4096 * i, [128, T], F32) for i in range(2)]
              xoff = (B0 + 147520) if isA else (B0 + 131136)
              XCo += [self.SB("xcx%d" % i, xoff + 4096 * i, [128, T], F32) for i in range(2)]
              nsrc = proj_residual(nk_out, (lambda k, hf, cat=cat: cat[:, k, hf * 512:(hf + 1) * 512]), cat_b, wout_d[l],
                                   WO, XCo, cur, cur_b, xd, self.xd_b, [(4, 5), (6, 7), (2, 3)],
                                   stats=True, prefetch=True)
              cur, cur_b = xd, self.xd_b
              chk(7)

              HB = self.SB("hb", ARENA - 8192 - 8192 - 256, [128, 2, 16], BF16)
              HBL = self.SB("hbl", ARENA - 8192 - 8192 - 192, [128, 16], BF16)
              HBR = self.SB("hbr", ARENA - 8192 - 8192 - 128, [128, 16], BF16)
              hb, hb_b = HB
              XSv = self.V(XSo, [128, KD, T], F32)
              xs_all = tuple(b_ for (_, b_) in XS)
              gl = gains[:, (4 + l) * 16:(4 + l) * 16 + 16]

              def early_halo(sources, hb=hb, hb_b=hb_b, HBL=HBL, HBR=HBR, gl=gl, l=l):
                  for ti, col in ((0, 0), (1, T - 1)):
                      self.dve(lambda e, ti=ti, col=col: e.scalar_tensor_tensor(
                          out=hb[:, ti, 0:12], in0=XSv[:, 0:12, col], scalar=RSTD[:, col:col + 1],
                          in1=gl[:, 0:12], op0=ALU.mult, op1=ALU.mult), xs_all + (RSTD_b, gains_b), (hb_b,))
                      for k in range(12, 16):
                          sa, sb_ = sources[k]
                          self.dve(lambda e, ti=ti, col=col, k=k, sa=sa: e.scalar_tensor_tensor(
                              out=hb[:, ti, k:k + 1], in0=sa[:, col:col + 1], scalar=RSTD[:, col:col + 1],
                              in1=gl[:, k:k + 1], op0=ALU.mult, op1=ALU.mult), (sb_, RSTD_b, gains_b), (hb_b,))
                  rec.dma("sp", hbloc[:, :], self.V(ARENA - 8192 - 8192 - 256, [128, 32], BF16), (hb_b,), (hblocb,), hb_b)
                  rec.collective(hbloc.ap().opt(), hball.ap().opt(), (hblocb,), (hballb,), cch, GROUPS)
                  rec.dma("sp", HBL[0], hball[0:128, 16:32], (hballb,), (HBL[1],), HBL[1])
                  rec.dma("sp", HBR[0], hball[128:256, 0:16], (hballb,), (HBR[1],), HBR[1])
              norm_to_HT(cur, cur_b, 4 + l, early=early_halo, preloaded=True, sources=nsrc)
              self.dve(lambda e, HBL=HBL: e.tensor_scalar(HT[:, :, 0], HBL[0], hm[:, 0:1], None, op0=ALU.mult),
                       (HBL[1], hm_b), (HT_b,))
              self.dve(lambda e, HBR=HBR: e.tensor_scalar(HT[:, :, TE - 1], HBR[0], hm[:, 1:2], None, op0=ALU.mult),
                       (HBR[1], hm_b), (HT_b,))
              G, G_b = self.SB("G", B0 + 32832, [128, 44, T], BF16)
              GB = [self.SB("Gk%d" % jj, B0 + 32832 + 2048 * jj, [128, T], BF16)[1] for jj in range(44)]
              WUP = [self.SB("wup%d" % i, B0 + 122944 + 4096 * i, [128, KD, 128], BF16) for i in range(5)]
              ASB = [self.SB("asb%d" % i, B0 + 143424 + 4128 * i, [128, TE], F32) for i in range(2)]
              CB = [self.SB("cb%d" % i, B0 + 151680 + 4096 * i, [128, T], F32) for i in range(3)]
              WDN0 = self.SB("wdn0", B0 + 163968, [128, 44, 128], BF16)
              WDN1 = self.SB("wdn1", B0 + 122944, [128, 44, 128], BF16)
              XCf = [self.SB("xc%d" % i, ARENA - 8192 - 8192 + 4096 * i, [128, T], F32) for i in range(2)]
              XCf += [self.SB("xcf%d" % i, B0 + 143424 + 4096 * i, [128, T], F32) for i in range(2)]
              wu = wup_d[l]
              chk(8)
              for u in range(5):
                  rec.dma("pool", WUP[u][0], wu[u, :, :, :], (), (WUP[u][1],), WUP[u][1])
              gate_c = None
              for u in range(NU):
                  w, w_b = WUP[u % 5]
                  bs = (0, 1, 2) if u % 2 == 0 else (3, 4, 5)
                  for k in range(KD):
                      for tg in range(3):
                          self.mm(self.PS(bs[tg], 342), w[:, k, :], HT[:, k, tg * 342:(tg + 1) * 342], k == 0, k == KD - 1,
                                  (w_b, HT_b), (self.bank[bs[tg]],))
                  if u + 5 < NU:
                      rec.dma("pool", w, wu[u + 5, :, :, :], (), (w_b,), w_b)
                  if u == 40:
                      rec.dma("pool", WDN0[0], wdn_d[l][0, :, :, :], (), (WDN0[1],), WDN0[1])
                  a, a_b = ASB[u % 2]
                  for tg in range(3):
                      self.act(a[:, tg * 342:(tg + 1) * 342], self.PS(bs[tg], 342), AF.Copy, (self.bank[bs[tg]],), (a_b,))
                  c, c_b = CB[u % 3]
                  cp = [convp[:, l, u, q:q + 1] for q in range(4)]
                  self.dve(lambda e, c=c, a=a, cp=cp: e.tensor_scalar(c, a[:, 1:1 + T], cp[1], cp[3], op0=ALU.mult,
                                                                       op1=ALU.add), (a_b, convp_b), (c_b,))
                  self.dve(lambda e, c=c, a=a, cp=cp: e.scalar_tensor_tensor(out=c, in0=a[:, 0:T], scalar=cp[0], in1=c,
                                                                              op0=ALU.mult, op1=ALU.add),
                           (a_b, c_b, convp_b), (c_b,))
                  self.dve(lambda e, c=c, a=a, cp=cp: e.scalar_tensor_tensor(out=c, in0=a[:, 2:2 + T], scalar=cp[2], in1=c,
                                                                              op0=ALU.mult, op1=ALU.add),
                           (a_b, c_b, convp_b), (c_b,))
                  if u % 2 == 0:
                      self.act(c, c, AF.Gelu, (c_b,), (c_b,))
                      gate_c = (c, c_b)
                  else:
                      gc, gc_b = gate_c
                      self.dve(lambda e, gc=gc, c=c, jj=u // 2, G=G: e.tensor_tensor(out=G[:, jj, :], in0=gc, in1=c, op=ALU.mult),
                               (gc_b, c_b), (GB[u // 2],))
              msrc = proj_residual(44, (lambda k, hf, G=G: G[:, k, hf * 512:(hf + 1) * 512]), (lambda k, GB=GB: GB[k]), wdn_d[l], [WDN0, WDN1],
                                   XCf, cur, cur_b, xd, self.xd_b, [(6, 7), (2, 3), (4, 5)], stats=True, preloaded=1,
                                   xs=XSM)

        except _Stop:
            pass

        yb = [Buf("y%d" % k) for k in range(KD)]
        if self.final_norm:
            XCy = [self.SB("xc%d" % i, ARENA - 8192 - 8192 + 4096 * i, [128, T], F32) for i in range(2)]

            def o(k, xs, xs_b, g):
                xc, xc_b = XCy[k % 2]
                self.dve(lambda e, xs=xs, g=g, xc=xc: e.scalar_tensor_tensor(out=xc, in0=xs, scalar=g, in1=RSTD,
                                                                             op0=ALU.mult, op1=ALU.mult),
                         (xs_b, RSTD_b, gains_b), (xc_b,))
                rec.dma("sp", y_out[k, :, :], xc, (xc_b,), (yb[k],), xc_b)
            norm(cur, cur_b, 12, o, preloaded=(msrc is not None), sources=msrc)
        else:
            XCy = [self.SB("xc%d" % i, ARENA - 8192 - 8192 + 4096 * i, [128, T], F32) for i in range(2)]
            for k in range(KD):
                xc, xc_b = XCy[k % 2]
                rec.dma("sp", xc, cur[k, :, :], (cur_b[k],), (xc_b,), xc_b)
                rec.dma("sp", y_out[k, :, :], xc, (xc_b,), (yb[k],), xc_b)
        rec.finish()
        rec.finalize()
        with nc.Block() as block:
            @block.tensor
            def _(e):
                rec.emit("pe", e)

            @block.scalar
            def _(e):
                rec.emit("act", e)

            @block.vector
            def _(e):
                rec.emit("dve", e)

            @block.gpsimd
            def _(e):
                rec.emit("pool", e)

            @block.sync
            def _(e):
                rec.emit("sp", e)
        self.stack.close()
        return nc


def _su(W, cols):
    Wc = W[:, cols]
    n = Wc.shape[1] // 512
    return np.ascontiguousarray(Wc.reshape(KD, 128, n, 512).transpose(2, 1, 0, 3))


def _masks(c):
    k = np.arange(128)[:, None]
    q = np.arange(128)[None, :]
    out = np.zeros((128, 2, 3, 3, 128), np.float32)
    for gi, d in enumerate((1, 4)):
        for v in range(3):
            for di in range(3):
                dj = 128 * (di - 1) + k - q
                ok = np.abs(dj) <= 64
                if v == 0 and di == 0 and c == 0:
                    ok = np.zeros_like(ok)
                if v == 2 and di == 2 and c == 1:
                    ok = np.zeros_like(ok)
                out[:, gi, v, di, :] = np.where(ok, d * np.abs(dj), BIG)
    dt01 = out.reshape(128, 2 * 3 * 384)
    kk = np.arange(128)[:, None]
    qq = np.arange(64)[None, :]
    dj = kk - (64 * c + qq)
    dt2 = np.where(np.abs(dj) <= 64, 16.0 * np.abs(dj), BIG).astype(np.float32)
    hm = np.zeros((128, 2), np.float32)
    hm[:, 0] = float(c)
    hm[:, 1] = float(1 - c)
    return dt01, dt2, hm


def _prep_shared(inp, layers):
    f = lambda a: np.asarray(a, dtype=np.float32)
    sh = {}
    g = np.concatenate([f(inp["mix_norm_g"]), f(inp["ffn_norm_g"]), f(inp["mem_norm_g"]),
                        f(inp["final_norm_g"])[None, :]], axis=0)
    sh["gains"] = np.ascontiguousarray(g.reshape(13, KD, 128).transpose(2, 0, 1).reshape(128, 13 * KD))
    cw, cb = f(inp["ffn_conv_w"]), f(inp["ffn_conv_b"])
    cp = np.concatenate([cw, cb[:, None, :]], axis=1)
    cp = cp.reshape(4, 4, 2, 44, 128)
    sh["convp"] = np.ascontiguousarray(cp.transpose(4, 0, 3, 2, 1).reshape(128, 4 * NU * 4))
    for l in layers:
        j = l // 2
        wkv = f(inp["w_mem_kv"][l])
        if l % 2 == 0:
            W = f(inp["a_w_in"][j])
            cols = np.concatenate([np.arange(1536, 3072), np.arange(3072, 4608), np.arange(0, 1536),
                                   np.arange(4608, 5120)])
            Wo = f(inp["a_w_out"][j])
            nk = 8
        else:
            W = f(inp["b_w_in"][j])
            cols = np.concatenate([np.arange(1536, 3072), np.arange(0, 1536), np.arange(3072, 3584)])
            Wo = f(inp["b_w_out"][j])
            nk = 16
            sh["gvb%d" % l] = np.ascontiguousarray(np.broadcast_to(f(inp["b_v_norm_g"][j])[None, :], (128, 1536)))
            sb = f(inp["b_s_bias"][j])
            sh["sbb%d" % l] = np.ascontiguousarray(np.broadcast_to(sb.reshape(1, 1536), (128, 1536)))
            ws = f(inp["b_w_s"][j])
            sh["wst%d" % l] = np.ascontiguousarray(ws.transpose(2, 0, 1).reshape(128, 1536))
        sh["win%d" % l] = np.concatenate([_su(W, cols), _su(wkv, np.arange(1024))], axis=0)
        sh["wout%d" % l] = np.ascontiguousarray(Wo.reshape(nk, 128, 16, 128).transpose(2, 1, 0, 3))
        Wu = f(inp["ffn_w_up"][l])
        sh["wup%d" % l] = np.ascontiguousarray(Wu.reshape(KD, 128, 2, 44, 128).transpose(3, 2, 1, 0, 4).reshape(NU, 128, KD, 128))
        Wd = f(inp["ffn_w_down"][l])
        sh["wdn%d" % l] = np.ascontiguousarray(Wd.reshape(44, 128, 16, 128).transpose(2, 1, 0, 3))
    return sh


def _run(inp, layers, final_norm, x_shards):
    b = Builder(layers, final_norm)
    nc = b.build()
    sh = _prep_shared(inp, layers)
    mem = np.asarray(inp["mem"], dtype=np.float32)
    in_maps = []
    for core in range(8):
        bi, c = core // 2, core % 2
        m = dict(sh)
        m["xT"] = x_shards[core]
        m["memT"] = np.ascontiguousarray(mem[bi].T.reshape(KD, 128, 256))
        m["dt01"], m["dt2"], m["hm"] = _masks(c)
        in_maps.append(m)
    res = run_bass_kernel_spmd(nc, in_maps, core_ids=list(range(8)))
    key = "yT" if final_norm else "xo"
    return [np.asarray(r[key]) for r in res.results]


def kernel(**inputs):
    x = np.asarray(inputs["x"], dtype=np.float32)
    shards = []
    for core in range(8):
        bi, c = core // 2, core % 2
        shards.append(np.ascontiguousarray(x[bi, c * T:(c + 1) * T, :].T.reshape(KD, 128, T)))
    outs = _run(inputs, list(range(N_LAYERS)), True, shards)
    y = np.zeros((4, 2048, D), np.float32)
    for core in range(8):
        bi, c = core // 2, core % 2
        y[bi, c * T:(c + 1) * T, :] = outs[core].reshape(D, T).T
    return y
```
